# Optimizing a Trainium2 kernel written in Bass

```python
import math
import numpy as np
import jax, jax.numpy as jnp
from jax import lax

D_MODEL = 1024
BATCH = 16
SEQ = 2048
DEPTH = 4

ROPE_THETA = 10000.0
NORM_EPS = 1e-6
Q_BLOCK = 128

A_HEADS = 8
A_NOPE = 64
A_ROPE = 32
A_KV_RANK = 128
A_V_DIM = 64
A_SCALE = (A_NOPE + A_ROPE) ** -0.5
IDX_HEADS = 8
IDX_DIM = 64
TOPK_MAX = 256

B_HEADS = 4
B_QK_DIM = 64
B_V_DIM = 2 * B_QK_DIM

C_HEADS = 16
C_GROUPS = 4
C_HPG = C_HEADS // C_GROUPS
C_DIM = 64
CMP_LEN = 32
CMP_STRIDE = 16
CMP_HIDDEN = 128
SEL_BLOCK = 64
SEL_TOPN = 16
WINDOW = 512
C_Q_BLOCK = 16

D_FF = -(-8 * D_MODEL // (3 * 256)) * 256

EVEN_SPLITS = [A_HEADS * A_NOPE, A_HEADS * A_ROPE, A_KV_RANK, A_ROPE, IDX_HEADS * IDX_DIM, IDX_DIM, IDX_HEADS,
               B_HEADS * 2 * B_QK_DIM, B_HEADS * 2 * B_QK_DIM, B_HEADS * B_V_DIM]
EVEN_IN = sum(EVEN_SPLITS)
ODD_SPLITS = [C_HEADS * C_DIM] + [C_GROUPS * C_DIM] * 6 + [C_HEADS * 3]
ODD_IN = sum(ODD_SPLITS)

kernel_name = 'hybrid_dsa_diff_nsa_trunk'


def rmsnorm(x, g):
    xf = x.astype(jnp.float32)
    y = xf * lax.rsqrt(jnp.mean(xf * xf, axis=-1, keepdims=True) + NORM_EPS)
    return (y * g.astype(jnp.float32)).astype(x.dtype)


def rope(x, pos):
    d = x.shape[-1]
    inv = ROPE_THETA ** (-jnp.arange(0, d, 2, dtype=jnp.float32) / d)
    ang = pos.astype(jnp.float32)[:, None] * inv[None, :]
    shape = (x.shape[1],) + (1,) * (x.ndim - 3) + (d // 2,)
    cos = jnp.cos(ang).reshape(shape)
    sin = jnp.sin(ang).reshape(shape)
    xf = x.astype(jnp.float32)
    x1, x2 = xf[..., :d // 2], xf[..., d // 2:]
    return jnp.concatenate([x1 * cos - x2 * sin, x2 * cos + x1 * sin], axis=-1).astype(x.dtype)


def safe_softmax(s, mask):
    s = jnp.where(mask, s, -jnp.inf)
    m = jnp.max(s, axis=-1, keepdims=True)
    m = jnp.where(jnp.isfinite(m), m, 0.0)
    e = jnp.exp(s - m)
    den = jnp.sum(e, axis=-1, keepdims=True)
    return e / jnp.where(den > 0, den, 1.0)


def split_cols(y, sizes):
    return jnp.split(y, np.cumsum(sizes)[:-1].tolist(), axis=-1)


def qslice(a, i, qb):
    return lax.dynamic_slice_in_dim(a, i * qb, qb, axis=1)


def map_blocks(fn, n_blocks):
    out = lax.map(fn, jnp.arange(n_blocks, dtype=jnp.int32))
    nb, b, qb = out.shape[:3]
    return jnp.moveaxis(out, 0, 1).reshape((b, nb * qb) + out.shape[3:])


def dsa_mixer(q_nope, q_rope, c_kv, k_rope, iq, ik, iw, kv_gain, w_uk, w_uv, pos):
    B, T = c_kv.shape[:2]
    q_nope = q_nope.reshape(B, T, A_HEADS, A_NOPE)
    q_rope = rope(q_rope.reshape(B, T, A_HEADS, A_ROPE), pos)
    c_kv = rmsnorm(c_kv, kv_gain)
    k_rope = rope(k_rope, pos)
    q_lat = jnp.einsum('bthd,hdc->bthc', q_nope, w_uk)
    qc = jnp.concatenate([q_lat, q_rope], axis=-1)
    kc = jnp.concatenate([c_kv, k_rope], axis=-1)
    iq = rope(iq.reshape(B, T, IDX_HEADS, IDX_DIM), pos)
    ik = rope(ik, pos)
    iw = iw * IDX_HEADS ** -0.5
    k_sel = min(TOPK_MAX, T // 4)

    def block(i):
        t = i * Q_BLOCK + jnp.arange(Q_BLOCK, dtype=jnp.int32)
        rel = jax.nn.relu(jnp.einsum('bqhd,bsd->bqhs', qslice(iq, i, Q_BLOCK), ik).astype(jnp.float32) * IDX_DIM ** -0.5)
        score = jnp.einsum('bqhs,bqh->bqs', rel, qslice(iw, i, Q_BLOCK).astype(jnp.float32))
        causal = pos[None, :] <= t[:, None]
        score = jnp.where(causal[None], score, -jnp.inf)
        _, idx = lax.top_k(score, k_sel)
        kg = jax.vmap(lambda k, j: k[j])(kc, idx)
        s = jnp.einsum('bqhc,bqkc->bqhk', qslice(qc, i, Q_BLOCK), kg).astype(jnp.float32) * A_SCALE
        valid = (idx <= t[None, :, None])[:, :, None, :]
        p = safe_softmax(s, valid)
        return jnp.einsum('bqhk,bqkc->bqhc', p.astype(kg.dtype), kg[..., :A_KV_RANK])

    o_lat = map_blocks(block, T // Q_BLOCK)
    o = jnp.einsum('bthc,hcd->bthd', o_lat, w_uv)
    return o.reshape(B, T, A_HEADS * A_V_DIM)


def diff_mixer(q, k, v, lam, subln_gain, lam_init, pos):
    B, T = q.shape[:2]
    q = rope(q.reshape(B, T, B_HEADS * 2, B_QK_DIM), pos).reshape(B, T, B_HEADS, 2, B_QK_DIM)
    k = rope(k.reshape(B, T, B_HEADS * 2, B_QK_DIM), pos).reshape(B, T, B_HEADS, 2, B_QK_DIM)
    v = v.reshape(B, T, B_HEADS, B_V_DIM)
    lf = lam.astype(jnp.float32)
    lam_full = jnp.exp(jnp.sum(lf[0] * lf[1])) - jnp.exp(jnp.sum(lf[2] * lf[3])) + lam_init

    def block(i):
        t = i * Q_BLOCK + jnp.arange(Q_BLOCK, dtype=jnp.int32)
        s = jnp.einsum('bqhmd,bshmd->bhmqs', qslice(q, i, Q_BLOCK), k).astype(jnp.float32) * B_QK_DIM ** -0.5
        causal = pos[None, :] <= t[:, None]
        p = safe_softmax(s, causal)
        a = p[:, :, 0] - lam_full * p[:, :, 1]
        return jnp.einsum('bhqs,bshe->bqhe', a.astype(v.dtype), v)

    o = map_blocks(block, T // Q_BLOCK)
    o = rmsnorm(o, subln_gain) * (1.0 - lam_init)
    return o.reshape(B, T, B_HEADS * B_V_DIM)


def compress(x, pe, w1, w2):
    B, T, G, d = x.shape
    chunks = x.reshape(B, T // CMP_STRIDE, CMP_STRIDE, G, d)
    blocks = jnp.concatenate([chunks[:, :-1], chunks[:, 1:]], axis=2) + pe[None, None, :, None, :]
    flat = blocks.transpose(0, 1, 3, 2, 4).reshape(B, -1, G, CMP_LEN * d)
    return jnp.einsum('bngf,fe->bnge', jax.nn.gelu(flat @ w1), w2)


def nsa_mixer(q, kc_raw, vc_raw, ks, vs, kw, vw, gates, pe, w1, w2, pos):
    B, T = q.shape[:2]
    kv = lambda a: a.reshape(B, T, C_GROUPS, C_DIM)
    q = rope(q.reshape(B, T, C_HEADS, C_DIM), pos).reshape(B, T, C_GROUPS, C_HPG, C_DIM)
    k_cmp = compress(rope(kv(kc_raw), pos), pe[0], w1[0], w2[0])
    v_cmp = compress(kv(vc_raw), pe[1], w1[1], w2[1])
    k_slc, v_slc = rope(kv(ks), pos), kv(vs)
    k_win, v_win = rope(kv(kw), pos), kv(vw)
    gates = jax.nn.sigmoid(gates.astype(jnp.float32)).reshape(B, T, C_GROUPS, C_HPG, 3).astype(q.dtype)
    n_cmp = T // CMP_STRIDE - 1
    n_blk = T // SEL_BLOCK
    top_n = min(SEL_TOPN, n_blk)
    cmp_end = jnp.arange(n_cmp, dtype=jnp.int32) * CMP_STRIDE + CMP_LEN - 1
    cs = np.arange(n_cmp) * CMP_STRIDE
    ss = np.arange(n_blk) * SEL_BLOCK
    overlap = jnp.asarray(((cs[:, None] < ss[None, :] + SEL_BLOCK) & (cs[:, None] + CMP_LEN > ss[None, :])).astype(np.float32))
    k_blocks = k_slc.reshape(B, n_blk, SEL_BLOCK, C_GROUPS, C_DIM).transpose(0, 3, 1, 2, 4)
    v_blocks = v_slc.reshape(B, n_blk, SEL_BLOCK, C_GROUPS, C_DIM).transpose(0, 3, 1, 2, 4)
    k_pad = jnp.pad(k_win, ((0, 0), (WINDOW, 0), (0, 0), (0, 0)))
    v_pad = jnp.pad(v_win, ((0, 0), (WINDOW, 0), (0, 0), (0, 0)))
    blk_ids = jnp.arange(n_blk, dtype=jnp.int32)
    scale = C_DIM ** -0.5
    gather = jax.vmap(jax.vmap(lambda a, j: a[j]))

    def block(i):
        t = i * C_Q_BLOCK + jnp.arange(C_Q_BLOCK, dtype=jnp.int32)
        qb = qslice(q, i, C_Q_BLOCK)
        s = jnp.einsum('bqghd,bngd->bqghn', qb, k_cmp).astype(jnp.float32) * scale
        p_cmp = safe_softmax(s, (cmp_end[None, :] <= t[:, None])[:, None, None, :])
        o_cmp = jnp.einsum('bqghn,bngd->bqghd', p_cmp.astype(v_cmp.dtype), v_cmp)
        imp = jnp.einsum('bqghn,nj->bqgj', p_cmp, overlap)
        cur = t // SEL_BLOCK
        forced = (blk_ids[None, :] == 0) | (blk_ids[None, :] == cur[:, None]) | (blk_ids[None, :] == cur[:, None] - 1)
        admissible = blk_ids[None, :] * SEL_BLOCK <= t[:, None]
        imp = jnp.where(forced[:, None, :], jnp.inf, imp)
        imp = jnp.where(admissible[:, None, :], imp, -jnp.inf)
        _, sel = lax.top_k(imp, top_n)
        sel = sel.transpose(0, 2, 1, 3)
        kg = gather(k_blocks, sel)
        vg = gather(v_blocks, sel)
        s = jnp.einsum('bqghd,bgqnld->bqghnl', qb, kg).astype(jnp.float32) * scale
        tok = sel[..., None] * SEL_BLOCK + jnp.arange(SEL_BLOCK, dtype=jnp.int32)
        m_slc = (tok <= t[None, None, :, None, None]).transpose(0, 2, 1, 3, 4)
        m_slc = m_slc.reshape(B, C_Q_BLOCK, C_GROUPS, 1, top_n * SEL_BLOCK)
        p = safe_softmax(s.reshape(B, C_Q_BLOCK, C_GROUPS, C_HPG, top_n * SEL_BLOCK), m_slc)
        p = p.reshape(B, C_Q_BLOCK, C_GROUPS, C_HPG, top_n, SEL_BLOCK)
        o_slc = jnp.einsum('bqghnl,bgqnld->bqghd', p.astype(vg.dtype), vg)
        kwb = lax.dynamic_slice_in_dim(k_pad, i * C_Q_BLOCK, WINDOW + C_Q_BLOCK, axis=1)
        vwb = lax.dynamic_slice_in_dim(v_pad, i * C_Q_BLOCK, WINDOW + C_Q_BLOCK, axis=1)
        kpos = i * C_Q_BLOCK - WINDOW + jnp.arange(WINDOW + C_Q_BLOCK, dtype=jnp.int32)
        dist = t[:, None] - kpos[None, :]
        m_win = ((dist >= 0) & (dist < WINDOW) & (kpos[None, :] >= 0))[:, None, None, :]
        s = jnp.einsum('bqghd,bsgd->bqghs', qb, kwb).astype(jnp.float32) * scale
        p = safe_softmax(s, m_win)
        o_win = jnp.einsum('bqghs,bsgd->bqghd', p.astype(vwb.dtype), vwb)
        g = qslice(gates, i, C_Q_BLOCK)
        return g[..., 0:1] * o_cmp + g[..., 1:2] * o_slc + g[..., 2:3] * o_win

    o = map_blocks(block, T // C_Q_BLOCK)
    return o.reshape(B, T, C_HEADS * C_DIM)


def swiglu(h, w_gate, w_up, w_down):
    return (jax.nn.silu(h @ w_gate) * (h @ w_up)) @ w_down


def _w(key, shape, fan_in, gain=1.0):
    return jax.random.normal(key, shape, jnp.float32) * (gain * fan_in ** -0.5)


def _gain(key, shape):
    return 1.0 + 0.02 * jax.random.normal(key, shape, jnp.float32)


def setup_inputs(seed: int = 0) -> dict:
    key = jax.random.key(seed)
    k = jax.random.split(key, 19)
    n_even = (DEPTH + 1) // 2
    n_odd = DEPTH // 2
    out_gain = (2 * DEPTH) ** -0.5
    return {
        'x': jax.random.normal(k[0], (BATCH, SEQ, D_MODEL), jnp.float32),
        'norm_mix': _gain(k[1], (DEPTH, D_MODEL)),
        'norm_ffn': _gain(k[2], (DEPTH, D_MODEL)),
        'norm_final': _gain(k[3], (D_MODEL,)),
        'ev_w_in': _w(k[4], (n_even, D_MODEL, EVEN_IN), D_MODEL),
        'ev_kv_gain': _gain(k[5], (n_even, A_KV_RANK)),
        'ev_w_uk': _w(k[6], (n_even, A_HEADS, A_NOPE, A_KV_RANK), A_KV_RANK),
        'ev_w_uv': _w(k[7], (n_even, A_HEADS, A_KV_RANK, A_V_DIM), A_KV_RANK),
        'ev_lambda': 0.1 * jax.random.normal(k[8], (n_even, 4, B_QK_DIM), jnp.float32),
        'ev_subln': _gain(k[9], (n_even, B_V_DIM)),
        'ev_w_out': _w(k[10], (n_even, D_MODEL, D_MODEL), D_MODEL, out_gain),
        'od_w_in': _w(k[11], (n_odd, D_MODEL, ODD_IN), D_MODEL),
        'od_cmp_pe': 0.1 * jax.random.normal(k[12], (n_odd, 2, CMP_LEN, C_DIM), jnp.float32),
        'od_cmp_w1': _w(k[13], (n_odd, 2, CMP_LEN * C_DIM, CMP_HIDDEN), CMP_LEN * C_DIM),
        'od_cmp_w2': _w(k[14], (n_odd, 2, CMP_HIDDEN, C_DIM), CMP_HIDDEN),
        'od_w_out': _w(k[15], (n_odd, D_MODEL, D_MODEL), D_MODEL, out_gain),
        'ffn_w_gate': _w(k[16], (DEPTH, D_MODEL, D_FF), D_MODEL),
        'ffn_w_up': _w(k[17], (DEPTH, D_MODEL, D_FF), D_MODEL),
        'ffn_w_down': _w(k[18], (DEPTH, D_FF, D_MODEL), D_FF, out_gain),
    }


def reference(x, norm_mix, norm_ffn, norm_final,
              ev_w_in, ev_kv_gain, ev_w_uk, ev_w_uv, ev_lambda, ev_subln, ev_w_out,
              od_w_in, od_cmp_pe, od_cmp_w1, od_cmp_w2, od_w_out,
              ffn_w_gate, ffn_w_up, ffn_w_down):
    pos = jnp.arange(x.shape[1], dtype=jnp.int32)
    for layer in range(DEPTH):
        j = layer // 2
        h = rmsnorm(x, norm_mix[layer])
        if layer % 2 == 0:
            y = h @ ev_w_in[j]
            qa_nope, qa_rope, c_kv, ka_rope, iq, ik, iw, qb, kb, vb = split_cols(y, EVEN_SPLITS)
            o_a = dsa_mixer(qa_nope, qa_rope, c_kv, ka_rope, iq, ik, iw, ev_kv_gain[j], ev_w_uk[j], ev_w_uv[j], pos)
            lam_init = 0.8 - 0.6 * math.exp(-0.3 * layer)
            o_b = diff_mixer(qb, kb, vb, ev_lambda[j], ev_subln[j], lam_init, pos)
            mix = jnp.concatenate([o_a, o_b], axis=-1) @ ev_w_out[j]
        else:
            y = h @ od_w_in[j]
            qc, kc_raw, vc_raw, ks, vs, kw, vw, gc = split_cols(y, ODD_SPLITS)
            o_c = nsa_mixer(qc, kc_raw, vc_raw, ks, vs, kw, vw, gc, od_cmp_pe[j], od_cmp_w1[j], od_cmp_w2[j], pos)
            mix = o_c @ od_w_out[j]
        x = x + mix
        x = x + swiglu(rmsnorm(x, norm_ffn[layer]), ffn_w_gate[layer], ffn_w_up[layer], ffn_w_down[layer])
    return rmsnorm(x, norm_final)
```

```python
import math
from contextlib import ExitStack
import numpy as np
import ml_dtypes
import concourse.bass as bass
import concourse.mybir as mybir
from concourse.bass_utils import run_bass_kernel_spmd

F32 = mybir.dt.float32
BF16 = mybir.dt.bfloat16
AF = mybir.ActivationFunctionType
ALU = mybir.AluOpType
AX = mybir.AxisListType

T = 2048
D = 1024
NTT = 16
NCH = 4
DFF = 2816
NFF = 22
EPS = 1e-6
NEG = -1.0e30
NEGREP = -3.0e38
A_SCALE = 96 ** -0.5
N_CORES = 8
DEPTH = 4


class Src:
    def __init__(self, sem, name):
        self.sem = sem
        self.cnt = 0
        self.name = name


class Eng:
    def __init__(self, h, src, name, is_pe=False):
        self.h = h
        self.src = src
        self.name = name
        self.seen = {}
        self.is_pe = is_pe


class Buf:
    __slots__ = ("t", "w", "r")

    def __init__(self, t):
        self.t = t
        self.w = None
        self.r = {}

    def __getitem__(self, k):
        return self.t[k]


class Sched:
    ROT = 30000

    def __init__(self, nc, n_dma_slots=32):
        self.nc = nc
        self._stack = ExitStack()
        self._n = 0
        self.pe = Eng(nc.tensor, self._mk("pe"), "pe", is_pe=True)
        self.act = Eng(nc.scalar, self._mk("act"), "act")
        self.dve = Eng(nc.vector, self._mk("dve"), "dve")
        self.pool = Eng(nc.gpsimd, self._mk("pool"), "pool")
        self.sp = Eng(nc.sync, self._mk("sp"), "sp")
        self.engs = [self.pe, self.act, self.dve, self.pool, self.sp]
        self.slots = [self._mk(f"dma{i}") for i in range(n_dma_slots)]
        self.slot_i = 0
        self.n_ins = 0
        self.all_srcs = [e.src for e in self.engs] + list(self.slots)

    def _mk(self, name):
        self._n += 1
        s = self._stack.enter_context(self.nc.semaphore(f"s{self._n}_{name}"))
        return Src(s, name)

    def close(self):
        self._stack.close()

    def _wait(self, eng, s, c):
        if eng.seen.get(s, 0) >= c:
            return
        if c > s.cnt:
            self._force(s)
            assert c <= s.cnt
        eng.h.wait_ge(s.sem, c)
        eng.seen[s] = c
        self.n_ins += 1

    def _deps(self, eng, reads, writes):
        deps = {}
        for b in reads:
            if b.w is not None:
                s, c = b.w
                if deps.get(s, 0) < c:
                    deps[s] = c
        for b in writes:
            if b.w is not None:
                s, c = b.w
                if deps.get(s, 0) < c:
                    deps[s] = c
            for s, c in b.r.items():
                if deps.get(s, 0) < c:
                    deps[s] = c
        for s, c in deps.items():
            if eng.is_pe and s is eng.src:
                continue
            self._wait(eng, s, c)

    def _force(self, s):
        if getattr(s, "pending", False):
            s.last_ins.then_inc(s.sem, 1)
            s.cnt += 1
            s.pending = False

    def op(self, eng, fn, reads=(), writes=(), sig=True):
        if eng.src.cnt >= self.ROT:
            self._force(eng.src)
            new = self._mk(eng.name)
            self.all_srcs.append(new)
            eng.src = new
        self._deps(eng, reads, writes)
        ins = fn(eng.h)
        s = eng.src
        if sig:
            s.cnt += 1
            ins.then_inc(s.sem, 1)
            s.pending = False
            c = s.cnt
        else:
            s.pending = True
            s.last_ins = ins
            c = s.cnt + 1
        for b in reads:
            b.r[s] = c
        for b in writes:
            b.w = (s, c)
            b.r = {}
        self.n_ins += 1
        return ins

    def dma(self, eng, out, in_, reads=(), writes=(), **kw):
        slot = self.slots[self.slot_i]
        self.slot_i = (self.slot_i + 1) % len(self.slots)
        if slot.cnt >= self.ROT:
            new = self._mk("dma")
            self.all_srcs.append(new)
            self.slots[(self.slot_i - 1) % len(self.slots)] = new
            self._wait(eng, slot, slot.cnt)
            slot = new
        self._deps(eng, reads, writes)
        if slot.cnt > 0:
            self._wait(eng, slot, slot.cnt)
        ins = eng.h.dma_start(out=out, in_=in_, **kw)
        slot.cnt += 16
        ins.then_inc(slot.sem, 16)
        c = slot.cnt
        for b in reads:
            b.r[slot] = c
        for b in writes:
            b.w = (slot, c)
            b.r = {}
        self.n_ins += 1
        return ins

    def barrier(self):
        for s in self.all_srcs:
            self._force(s)
        for e in self.engs:
            for s in self.all_srcs:
                if s.cnt > 0:
                    if e.is_pe and s is e.src:
                        continue
                    self._wait(e, s, s.cnt)


def chunkify(W, mc):
    K, n = W.shape
    kk = K // 128
    nch = n // mc
    return np.ascontiguousarray(W.reshape(kk, 128, nch, mc).transpose(2, 1, 0, 3).reshape(nch, 128, kk * mc))


def rope_swap_cols(ncols, d):
    idx = np.arange(ncols)
    h = idx // d
    i = idx % d
    return h * d + (i + d // 2) % d


def rope_tables(d, nrep):
    inv = (10000.0 ** (-np.arange(0, d, 2, dtype=np.float32) / np.float32(d))).astype(np.float32)
    pos = np.arange(T, dtype=np.float32)
    ang = (pos[:, None] * inv[None, :]).astype(np.float32)
    cos = np.cos(ang).astype(np.float32).T
    sin = np.sin(ang).astype(np.float32).T
    C = np.concatenate([cos, cos], 0)
    Sg = np.concatenate([-sin, sin], 0)
    return (np.ascontiguousarray(np.tile(C, (nrep, 1))), np.ascontiguousarray(np.tile(Sg, (nrep, 1))))


EVEN_SPLITS = [512, 256, 128, 32, 512, 64, 8, 512, 512, 512]
ODD_SPLITS = [1024] + [256] * 6 + [48]


def host_consts():
    c = {}
    c["identF"] = np.eye(128, dtype=np.float32)
    k = np.arange(128)[:, None]
    q = np.arange(128)[None, :]
    c["causT"] = (k <= q).astype(np.float32)
    c["antiT"] = (k > q).astype(np.float32)
    c["negcaus"] = np.where(q <= k, 0.0, NEG).astype(np.float32)
    qq = np.arange(512)[None, None, :]
    ii = np.arange(4)[None, :, None]
    kk = np.arange(128)[:, None, None]
    c["CM"] = ((128 * ii + kk) <= qq).astype(np.float32).reshape(128, 4 * 512)
    c["WM"] = (qq < (128 * ii + kk)).astype(np.float32).reshape(128, 4 * 512)
    n = np.arange(128)[:, None]
    t = np.arange(T)[None, :]
    c["maskcmpT"] = ((16 * n + 31 <= t) & (n < 127)).astype(np.float32)
    jb = np.arange(32)[None, :]
    c["ovl"] = ((16 * n < 64 * jb + 64) & (16 * n + 32 > 64 * jb) & (n < 127)).astype(np.float32)
    sa = np.zeros((128, 16, 32), np.float32)
    for gq in range(16):
        tt_ = 128 * gq + np.arange(128)
        cur = (tt_ // 64)[:, None]
        j2 = np.arange(32)[None, :]
        v = np.zeros((128, 32), np.float64)
        v = np.where(j2 > cur, NEG * (1.0 + j2 / 64.0), v)
        v = v + np.where(j2 == cur, 1.0e30, 0.0) + np.where(j2 == cur - 1, 2.0e30, 0.0)
        v = v + np.where((j2 == 0), 4.0e30, 0.0)
        sa[:, gq, :] = v.astype(np.float32)
    c["SELADD"] = sa.reshape(128, 16 * 32)
    eb = np.zeros((32, 16, 128), np.float32)
    for kt in range(16):
        eb[2 * kt, kt, 0:64] = 1.0
        eb[2 * kt + 1, kt, 64:128] = 1.0
    c["EXPB"] = eb.reshape(32, 16 * 128)
    sb_ = np.zeros((48, 48, 64), np.float32)
    for r in range(48):
        sb_[r, r, :] = 1.0
    c["SELB"] = sb_.reshape(48, 48 * 64)
    C64, S64 = rope_tables(64, 2)
    C32, S32 = rope_tables(32, 2)
    c["C64"], c["S64"], c["C32"], c["S32"] = C64, S64, C32, S32
    return c


def prep_weights(inp):
    w = {}
    g = np.zeros((128, 72), np.float32)
    for l in range(DEPTH):
        g[:, l * 8:(l + 1) * 8] = inp["norm_mix"][l].reshape(8, 128).T
        g[:, 32 + l * 8:32 + (l + 1) * 8] = inp["norm_ffn"][l].reshape(8, 128).T
    g[:, 64:72] = inp["norm_final"].reshape(8, 128).T
    w["gains"] = g
    for l in range(DEPTH):
        w[f"f{l}_wg"] = chunkify(inp["ffn_w_gate"][l], 128)
        w[f"f{l}_wu"] = chunkify(inp["ffn_w_up"][l], 128)
        Wd = inp["ffn_w_down"][l]
        w[f"f{l}_wd"] = np.ascontiguousarray(Wd.reshape(NFF, 128, 8, 128).transpose(2, 1, 0, 3).reshape(8, 128, NFF * 128))
    for j in range(2):
        W = inp["ev_w_in"][j]
        offs = np.cumsum([0] + EVEN_SPLITS)
        qn, qr, ckv, kr, iq, ik, iw, qb, kb, vb = [W[:, offs[i]:offs[i + 1]] for i in range(10)]
        p = f"e{j}_"
        w[p + "ckv"] = chunkify(ckv, 128)
        kr2 = np.concatenate([kr, kr], 1)
        w[p + "kr"] = chunkify(kr2, 64)
        w[p + "krs"] = chunkify(kr2[:, rope_swap_cols(64, 32)], 64)
        ik2 = np.concatenate([ik, ik], 1)
        w[p + "ik"] = chunkify(ik2, 128)
        w[p + "iks"] = chunkify(ik2[:, rope_swap_cols(128, 64)], 128)
        w[p + "qn"] = chunkify(qn, 128)
        w[p + "qr"] = chunkify(qr, 64)
        w[p + "qrs"] = chunkify(qr[:, rope_swap_cols(256, 32)], 64)
        w[p + "iq"] = chunkify(iq, 128)
        w[p + "iqs"] = chunkify(iq[:, rope_swap_cols(512, 64)], 128)
        w[p + "iw"] = chunkify(iw, 8)
        w[p + "kb"] = chunkify(kb, 128)
        w[p + "kbs"] = chunkify(kb[:, rope_swap_cols(512, 64)], 128)
        w[p + "qb"] = chunkify(qb, 128)
        w[p + "qbs"] = chunkify(qb[:, rope_swap_cols(512, 64)], 128)
        w[p + "vb"] = chunkify(vb, 512)
        uk = inp["ev_w_uk"][j]
        w[p + "uk"] = np.ascontiguousarray(uk.reshape(4, 2, 64, 128).transpose(1, 2, 0, 3).reshape(128, 4 * 128))
        uv = inp["ev_w_uv"][j]
        uvp = np.zeros((128, 8, 128), np.float32)
        for h in range(8):
            uvp[:, h, (h % 2) * 64:(h % 2) * 64 + 64] = uv[h]
        w[p + "uvp"] = uvp.reshape(128, 8 * 128)
        wo = inp["ev_w_out"][j]
        w[p + "wo"] = np.ascontiguousarray(wo.reshape(8, 128, 1024).transpose(1, 0, 2).reshape(128, 8 * 1024))
        w[p + "kvg"] = np.ascontiguousarray(inp["ev_kv_gain"][j].reshape(128, 1))
        w[p + "sub"] = np.ascontiguousarray(inp["ev_subln"][j].reshape(128, 1))
        w[p + "lam"] = np.ascontiguousarray(inp["ev_lambda"][j].reshape(1, 256))
    for j in range(2):
        W = inp["od_w_in"][j]
        offs = np.cumsum([0] + ODD_SPLITS)
        qc, kc, vc, ks, vs, kw, vw, gc = [W[:, offs[i]:offs[i + 1]] for i in range(8)]
        p = f"o{j}_"
        perm = []
        for gg in range(2):
            for i in range(4):
                for hf in range(2):
                    hd = (2 * gg + hf) * 4 + i
                    perm.extend(range(hd * 64, hd * 64 + 64))
        perm = np.array(perm)
        qp = qc[:, perm]
        w[p + "q"] = chunkify(qp, 128)
        w[p + "qs"] = chunkify(qp[:, rope_swap_cols(1024, 64)], 128)
        for nm, a in (("kc", kc), ("ks", ks), ("kw", kw)):
            w[p + nm] = chunkify(a, 128)
            w[p + nm + "s"] = chunkify(a[:, rope_swap_cols(256, 64)], 128)
        w[p + "vc"] = chunkify(vc, 128)
        w[p + "vs"] = chunkify(vs, 256)
        w[p + "vw"] = chunkify(vw, 256)
        w[p + "g"] = chunkify(gc, 48)
        for kv, nm in ((0, "k"), (1, "v")):
            w1 = inp["od_cmp_w1"][j][kv]
            a = w1.reshape(32, 64, 128).transpose(1, 0, 2).reshape(64, 32 * 128)
            w[p + "w1" + nm] = np.ascontiguousarray(np.concatenate([a, a], 0))
            pe = inp["od_cmp_pe"][j][kv]
            w[p + "pe" + nm] = np.ascontiguousarray(np.concatenate([pe.T, pe.T], 0))
            w[p + "w2" + nm] = np.ascontiguousarray(inp["od_cmp_w2"][j][kv])
        wo = inp["od_w_out"][j]
        w[p + "wo"] = chunkify(wo[perm, :], 128)
    return w


class Prog:
    def __init__(self, nseq, layers, wshapes, cshapes, parts=("mix", "ffn"), final=True):
        self.nseq = nseq
        nc = self.nc = bass.Bass("TRN2", target_bir_lowering=False)
        self.S = Sched(nc)
        self.stk = [ExitStack()]
        self._u = 0
        self.dram = {}
        self.x = nc.dram_tensor("x", [nseq, T, D], F32, kind="ExternalInput").ap()
        self.out = nc.dram_tensor("out", [nseq, T, D], F32, kind="ExternalOutput").ap()
        for k, shp in list(wshapes.items()) + list(cshapes.items()):
            self.dram[k] = nc.dram_tensor(k, list(shp), F32, kind="ExternalInput").ap()
        self.layers = layers
        self.parts = parts
        self.final = final
        self.build()
        self.S.close()

    def sb(self, name, shape, dt):
        self._u += 1
        t = self.stk[-1].enter_context(self.nc.sbuf_tensor(f"{name}_{self._u}", list(shape), dt))
        return Buf(t)

    def push(self):
        self.stk.append(ExitStack())

    def pop(self):
        self.S.barrier()
        self.stk.pop().close()

    def ring(self, name, n, shape, dt):
        bufs = [self.sb(f"{name}{i}", shape, dt) for i in range(n)]
        state = {"i": 0}

        def nxt():
            b = bufs[state["i"] % n]
            state["i"] += 1
            return b
        return nxt

    def mm(self, ps, out_ap, lhsT_ap, rhs_ap, start, stop, reads):
        self.S.op(self.S.pe, lambda e: e.matmul(out_ap, lhsT=lhsT_ap, rhs=rhs_ap, start=start, stop=stop),
                  reads=reads, writes=[ps], sig=bool(stop))

    def act(self, out_ap, in_ap, func, reads, writes, **kw):
        self.S.op(self.S.act, lambda e: e.activation(out=out_ap, in_=in_ap, func=func, **kw), reads=reads, writes=writes)

    def tt(self, out_ap, a_ap, b_ap, op, reads, writes):
        self.S.op(self.S.dve, lambda e: e.tensor_tensor(out=out_ap, in0=a_ap, in1=b_ap, op=op), reads=reads, writes=writes)

    def ts(self, out_ap, a_ap, s1, op0, reads, writes, s2=None, op1=None):
        if op1 is None:
            self.S.op(self.S.dve, lambda e: e.tensor_scalar(out=out_ap, in0=a_ap, scalar1=s1, scalar2=None, op0=op0),
                      reads=reads, writes=writes)
        else:
            self.S.op(self.S.dve, lambda e: e.tensor_scalar(out=out_ap, in0=a_ap, scalar1=s1, scalar2=s2, op0=op0, op1=op1),
                      reads=reads, writes=writes)

    def stt(self, out_ap, a_ap, sc, b_ap, op0, op1, reads, writes):
        self.S.op(self.S.dve, lambda e: e.scalar_tensor_tensor(out=out_ap, in0=a_ap, scalar=sc, in1=b_ap, op0=op0, op1=op1),
                  reads=reads, writes=writes)

    def vcopy(self, out_ap, in_ap, reads, writes):
        self.S.op(self.S.dve, lambda e: e.tensor_copy(out=out_ap, in_=in_ap), reads=reads, writes=writes)

    def recip(self, out_ap, in_ap, reads, writes):
        self.S.op(self.S.dve, lambda e: e.reciprocal(out=out_ap, in_=in_ap), reads=reads, writes=writes)

    def load_w(self, dst, dst_ap, src_ap):
        self.S.dma(self.S.pool, dst_ap, src_ap, writes=[dst])

    def load_f(self, dst, dst_ap, src_ap):
        self.S.dma(self.S.sp, dst_ap, src_ap, writes=[dst])

    def pipeline(self, steps, s_fn, p_fn, skew=2):
        pend = []
        for st in steps:
            pend.append((st, s_fn(st)))
            if len(pend) > skew:
                a, b = pend.pop(0)
                p_fn(a, b)
        while pend:
            a, b = pend.pop(0)
            p_fn(a, b)

    def arecip(self, out_buf, out_ap, in_buf, in_ap, bias=0.0):
        if bias != 0.0:
            self.act(out_ap, in_ap, AF.Ln, [in_buf], [out_buf], bias=bias)
        else:
            self.act(out_ap, in_ap, AF.Ln, [in_buf], [out_buf])
        self.act(out_ap, out_ap, AF.Exp, [out_buf], [out_buf], scale=-1.0)

    def psn(self):
        rb = self.ring_banks
        b = self.PS[rb[self.ps_i % len(rb)]]
        self.ps_i += 1
        return b

    def set_ring(self, banks):
        self.ring_banks = list(banks)

    def load_split(self, dst, dst_ap, src_ap, n, cast=True):
        A = dst_ap.shape[1]
        step = (A + n - 1) // n
        for a0 in range(0, A, step):
            a1 = min(A, a0 + step)
            if cast:
                self.load_w(dst, dst_ap[:, a0:a1], src_ap[:, a0:a1])
            else:
                self.load_f(dst, dst_ap[:, a0:a1], src_ap[:, a0:a1])

    def proj(self, ps, w, wcols, M, hT, hcols, N):
        for k in range(8):
            self.mm(ps, ps[0:M, 0:N], w[:, k, wcols], hT[:, k, hcols], k == 0, k == 7, [w, hT])

    def norm_chunk(self, c, gcol, dst, dcol0, f32_out=False):
        S = self.S
        xb = self.xTb[c]
        cols = slice(c * 512, (c + 1) * 512)
        ss = self.psn()
        for k in range(8):
            sq = self.sqring()
            self.act(sq[:, :], self.xT[:, k, cols], AF.Square, [xb], [sq])
            self.mm(ss, ss[:, :], self.onesB[:, :], sq[:, :], k == 0, k == 7, [self.onesB, sq])
        rs = self.rsring()
        self.act(rs[:, :], ss[:, :], AF.Sqrt, [ss], [rs], bias=EPS, scale=1.0 / D)
        self.recip(rs[:, :], rs[:, :], [rs], [rs])
        for k in range(8):
            self.stt(dst[:, k, dcol0:dcol0 + 512], self.xT[:, k, cols], self.gains[:, gcol + k:gcol + k + 1], rs[:, :],
                     ALU.mult, ALU.mult, [xb, self.gains, rs], [dst])

    def rope_comb(self, psA, psB, Ct, St, tcols, M, dst, dst_ap):
        t1 = self.t1ring()
        t2 = self.t1ring()
        self.tt(t1[0:M, :], psA[0:M, :], Ct[0:M, tcols], ALU.mult, [psA, Ct], [t1])
        self.tt(t2[0:M, :], psB[0:M, :], St[0:M, tcols], ALU.mult, [psB, St], [t2])
        self.tt(dst_ap, t1[0:M, :], t2[0:M, :], ALU.add, [t1, t2], [dst])

    def pnorm(self, src_ps, gain_ap, gain_buf, dst, dst_ap, nfeat, ss_bank=None):
        sq = self.sqring()
        self.act(sq[:, :], src_ps[:, :], AF.Square, [src_ps], [sq])
        ss = self.psn() if ss_bank is None else ss_bank
        self.mm(ss, ss[:, :], self.onesB[:, :], sq[:, :], True, True, [self.onesB, sq])
        rs = self.rsring()
        self.act(rs[:, :], ss[:, :], AF.Sqrt, [ss], [rs], bias=EPS, scale=1.0 / nfeat)
        self.recip(rs[:, :], rs[:, :], [rs], [rs])
        self.stt(dst_ap, src_ps[:, :], gain_ap, rs[:, :], ALU.mult, ALU.mult, [src_ps, gain_buf, rs], [dst])

    def out_proj(self, wo, koff, oT, c):
        xb = self.xTb[c]
        for m in range(8):
            ps = self.psn()
            for k in range(4):
                self.mm(ps, ps[:, :], wo[:, koff + k, m * 128:(m + 1) * 128], oT[:, k, :], k == 0, k == 3, [wo, oT])
            self.tt(self.xT[:, m, c * 512:(c + 1) * 512], self.xT[:, m, c * 512:(c + 1) * 512], ps[:, :], ALU.add, [xb, ps], [xb])

    def load_x(self, s):
        for tt in range(NTT):
            xin = self.xinring()
            self.S.dma(self.S.sp, xin[:, :], self.x[s, tt * 128:(tt + 1) * 128, :], writes=[xin])
            xb = self.xTb[tt // 4]
            for half in range(2):
                ps = self.psn()
                for cc in range(4):
                    c8 = half * 4 + cc
                    self.S.op(self.S.pe, lambda e, ps=ps, cc=cc, c8=c8, xin=xin: e.transpose(
                        out=ps[:, cc * 128:(cc + 1) * 128], in_=xin[:, c8 * 128:(c8 + 1) * 128], identity=self.identF[:, :]),
                        reads=[xin, self.identF], writes=[ps])
                self.act(self.xT[:, half * 4:half * 4 + 4, tt * 128:(tt + 1) * 128],
                         ps[:, :].rearrange("p (c t) -> p c t", c=4), AF.Copy, [ps], [xb])

    def store_out(self, s):
        yT = self.sb("yT", [128, 8, 512], F32)
        for c in range(NCH):
            if self.final == "raw":
                for k in range(8):
                    self.vcopy(yT[:, k, :], self.xT[:, k, c * 512:(c + 1) * 512], [self.xTb[c]], [yT])
            else:
                self.norm_chunk(c, 64, yT, 0)
            for t4 in range(4):
                tt = c * 4 + t4
                yo = self.xinring()
                for half in range(2):
                    ps = self.psn()
                    for cc in range(4):
                        c8 = half * 4 + cc
                        self.S.op(self.S.pe, lambda e, ps=ps, cc=cc, c8=c8: e.transpose(
                            out=ps[:, cc * 128:(cc + 1) * 128], in_=yT[:, c8, t4 * 128:(t4 + 1) * 128], identity=self.identF[:, :]),
                            reads=[yT, self.identF], writes=[ps])
                    self.act(yo[:, half * 512:(half + 1) * 512], ps[:, :], AF.Copy, [ps], [yo])
                self.S.dma(self.S.sp, self.out[s, tt * 128:(tt + 1) * 128, :], yo[:, :], reads=[yo], writes=[self.outbuf])

    def ffn(self, l):
        self.push()
        hT = self.sb("hTf", [128, 8, 1024], BF16)
        actb = self.sb("actb", [128, NFF, 1024], BF16)
        wgr = self.ring("wg", 8, [128, 8, 128], BF16)
        wur = self.ring("wu", 8, [128, 8, 128], BF16)
        wdr = self.ring("wd", 3, [128, NFF, 128], BF16)
        sgr = self.ring("sg", 2, [128, 512], F32)
        wg_d = self.dram[f"f{l}_wg"]
        wu_d = self.dram[f"f{l}_wu"]
        wd_d = self.dram[f"f{l}_wd"]
        for half in range(2):
            for sub in range(2):
                self.norm_chunk(half * 2 + sub, 32 + l * 8, hT, sub * 512)
            for j in range(NFF):
                wg = wgr()
                wu = wur()
                self.load_w(wg, wg[:, :, :], wg_d[j].rearrange("p (k m) -> p k m", k=8))
                self.load_w(wu, wu[:, :, :], wu_d[j].rearrange("p (k m) -> p k m", k=8))
                for sub in range(2):
                    hc = slice(sub * 512, (sub + 1) * 512)
                    pg = self.psn()
                    pu = self.psn()
                    self.proj(pg, wg, slice(0, 128), 128, hT, hc, 512)
                    self.proj(pu, wu, slice(0, 128), 128, hT, hc, 512)
                    sg = sgr()
                    self.act(sg[:, :], pg[:, :], AF.Silu, [pg], [sg])
                    self.tt(actb[:, j, hc], sg[:, :], pu[:, :], ALU.mult, [sg, pu], [actb])
            for m in range(8):
                wd = wdr()
                self.load_split(wd, wd[:, :, :], wd_d[m].rearrange("p (j n) -> p j n", j=NFF), 2)
                for sub in range(2):
                    c = half * 2 + sub
                    hc = slice(sub * 512, (sub + 1) * 512)
                    ps = self.psn()
                    for j in range(NFF):
                        self.mm(ps, ps[:, :], wd[:, j, :], actb[:, j, hc], j == 0, j == NFF - 1, [wd, actb])
                    xc = slice(c * 512, (c + 1) * 512)
                    self.tt(self.xT[:, m, xc], self.xT[:, m, xc], ps[:, :], ALU.add, [self.xTb[c], ps], [self.xTb[c]])
        self.pop()

    def dsa(self, l, oTaF):
        j = l // 2
        p = f"e{j}_"
        dr = self.dram
        S = self.S
        gcol = l * 8
        self.push()
        ckvT = self.sb("ckvT", [128, T], BF16)
        ckvTok = self.sb("ckvTok", [128, NTT, 128], BF16)
        krT = self.sb("krT", [64, T], BF16)
        ikT = self.sb("ikT", [128, T], BF16)
        kvg = self.sb("kvg", [128, 1], F32)
        self.load_f(kvg, kvg[:, :], dr[p + "kvg"][:, :])
        self.push()
        hT = self.sb("hTk", [128, 8, T], BF16)
        wckv = self.sb("wckv", [128, 8, 128], BF16)
        wkr = self.sb("wkr", [128, 8, 64], BF16)
        wkrs = self.sb("wkrs", [128, 8, 64], BF16)
        wik = self.sb("wik", [128, 8, 128], BF16)
        wiks = self.sb("wiks", [128, 8, 128], BF16)
        for wt, nm, mc in ((wckv, "ckv", 128), (wkr, "kr", 64), (wkrs, "krs", 64), (wik, "ik", 128), (wiks, "iks", 128)):
            self.load_w(wt, wt[:, :, :], dr[p + nm][0].rearrange("p (k m) -> p k m", k=8))
        C64 = self.sb("C64", [128, T], F32)
        S64 = self.sb("S64", [128, T], F32)
        C32 = self.sb("C32", [64, T], F32)
        S32 = self.sb("S32", [64, T], F32)
        for tb, nm in ((C64, "C64"), (S64, "S64"), (C32, "C32"), (S32, "S32")):
            for hf in range(2):
                self.load_f(tb, tb[:, hf * 1024:(hf + 1) * 1024], dr[nm][:, hf * 1024:(hf + 1) * 1024])
        for c in range(NCH):
            self.norm_chunk(c, gcol, hT, c * 512)
        for c in range(NCH):
            cols = slice(c * 512, (c + 1) * 512)
            ps = self.psn()
            self.proj(ps, wckv, slice(0, 128), 128, hT, cols, 512)
            self.pnorm(ps, kvg[:, 0:1], kvg, ckvT, ckvT[:, cols], 128)
            pa = self.psn()
            pb = self.psn()
            self.proj(pa, wkr, slice(0, 64), 64, hT, cols, 512)
            self.proj(pb, wkrs, slice(0, 64), 64, hT, cols, 512)
            self.rope_comb(pa, pb, C32, S32, cols, 64, krT, krT[:, cols])
            pa = self.psn()
            pb = self.psn()
            self.proj(pa, wik, slice(0, 128), 128, hT, cols, 512)
            self.proj(pb, wiks, slice(0, 128), 128, hT, cols, 512)
            self.rope_comb(pa, pb, C64, S64, cols, 128, ikT, ikT[:, cols])
        for t8 in range(2):
            ps = self.psn()
            psb = ps[:, :].bitcast(BF16)
            for i in range(8):
                tt = t8 * 8 + i
                S.op(S.pe, lambda e, psb=psb, i=i, tt=tt: e.transpose(out=psb[:, i * 128:(i + 1) * 128], in_=ckvT[:, tt * 128:(tt + 1) * 128],
                                                                     identity=self.identB[:, :]), reads=[ckvT, self.identB], writes=[ps])
            self.act(ckvTok[:, t8 * 8:(t8 + 1) * 8, :], psb.rearrange("p (i t) -> p i t", i=8), AF.Copy, [ps], [ckvTok])
        self.pop()
        self.push()
        hTc = self.sb("hTc", [128, 8, 512], BF16)
        qlatT = self.sb("qlatT", [128, 8, 512], BF16)
        qrT = self.sb("qrT", [64, 4, 512], BF16)
        iqT = self.sb("iqT", [128, 4, 512], BF16)
        iwt = self.sb("iwt", [128, 4, 8], F32)
        Isc = self.sb("Isc", [128, T], F32)
        Wk = self.sb("Wk", [128, T], F32)
        Mq = self.sb("Mq", [128, T], BF16)
        MT = self.sb("MT", [128, NTT, 512], BF16)
        wuk = self.sb("wuk", [128, 4, 128], BF16)
        wuvp = self.sb("wuvp", [128, 8, 128], BF16)
        wiw = self.sb("wiw", [128, 8, 8], BF16)
        self.load_w(wuk, wuk[:, :, :], dr[p + "uk"][:, :].rearrange("p (j c) -> p j c", j=4))
        self.load_w(wuvp, wuvp[:, :, :], dr[p + "uvp"][:, :].rearrange("p (h c) -> p h c", h=8))
        self.load_w(wiw, wiw[:, :, :], dr[p + "iw"][0].rearrange("p (k m) -> p k m", k=8))
        wr = self.ring("wq", 3, [128, 8, 128], BF16)
        C64c = self.sb("C64c", [128, 512], F32)
        S64c = self.sb("S64c", [128, 512], F32)
        C32c = C64c
        S32c = S64c
        qnr = self.ring("qn", 2, [128, 512], BF16)
        relur = self.ring("relu", 2, [128, 512], F32)
        m8r = self.ring("m8", 3, [128, 8], F32)
        Ptr = self.ring("Pt", 3, [128, 512], BF16)
        olr = self.ring("oln", 2, [128, 512], BF16)
        c0 = slice(0, 512)
        for c in range(NCH):
            cols = slice(c * 512, (c + 1) * 512)
            self.norm_chunk(c, gcol, hTc, 0)
            self.load_f(C32c, C32c[0:64, :], dr["C32"][:, cols])
            self.load_f(S32c, S32c[0:64, :], dr["S32"][:, cols])
            for jj in range(4):
                wq = wr()
                self.load_w(wq, wq[:, :, :], dr[p + "qn"][jj].rearrange("p (k m) -> p k m", k=8))
                ps = self.psn()
                self.proj(ps, wq, slice(0, 128), 128, hTc, c0, 512)
                qn = qnr()
                self.act(qn[:, :], ps[:, :], AF.Copy, [ps], [qn])
                for hh in range(2):
                    h = 2 * jj + hh
                    ps2 = self.psn()
                    self.mm(ps2, ps2[:, :], wuk[64 * hh:64 * hh + 64, jj, :], qn[64 * hh:64 * hh + 64, :], True, True, [wuk, qn])
                    self.act(qlatT[:, h, :], ps2[:, :], AF.Copy, [ps2], [qlatT])
            for jj in range(4):
                wq = wr()
                wqs = wr()
                self.load_w(wq, wq[:, :, 0:64], dr[p + "qr"][jj].rearrange("p (k m) -> p k m", k=8))
                self.load_w(wqs, wqs[:, :, 0:64], dr[p + "qrs"][jj].rearrange("p (k m) -> p k m", k=8))
                pa = self.psn()
                pb = self.psn()
                self.proj(pa, wq, slice(0, 64), 64, hTc, c0, 512)
                self.proj(pb, wqs, slice(0, 64), 64, hTc, c0, 512)
                self.rope_comb(pa, pb, C32c, S32c, c0, 64, qrT, qrT[:, jj, :])
            self.load_f(C64c, C64c[:, :], dr["C64"][:, cols])
            self.load_f(S64c, S64c[:, :], dr["S64"][:, cols])
            for jj in range(4):
                wq = wr()
                wqs = wr()
                self.load_w(wq, wq[:, :, :], dr[p + "iq"][jj].rearrange("p (k m) -> p k m", k=8))
                self.load_w(wqs, wqs[:, :, :], dr[p + "iqs"][jj].rearrange("p (k m) -> p k m", k=8))
                pa = self.psn()
                pb = self.psn()
                self.proj(pa, wq, slice(0, 128), 128, hTc, c0, 512)
                self.proj(pb, wqs, slice(0, 128), 128, hTc, c0, 512)
                self.rope_comb(pa, pb, C64c, S64c, c0, 128, iqT, iqT[:, jj, :])
            for qb in range(4):
                ps = self.psn()
                for k in range(8):
                    self.mm(ps, ps[:, 0:8], hTc[:, k, qb * 128:(qb + 1) * 128], wiw[:, k, :], k == 0, k == 7, [hTc, wiw])
                self.vcopy(iwt[:, qb, :], ps[:, 0:8], [ps], [iwt])
            for qb in range(4):
                gq = 4 * c + qb
                n = (gq + 1) * 128
                qsl = slice(qb * 128, (qb + 1) * 128)
                nkc = (n + 511) // 512
                for kc in range(nkc):
                    ncol = min(512, n - kc * 512)
                    ks = slice(kc * 512, kc * 512 + ncol)
                    for h in range(8):
                        hh = h % 2
                        jj = h // 2
                        ps = self.psn()
                        self.mm(ps, ps[:, 0:ncol], iqT[64 * hh:64 * hh + 64, jj, qsl], ikT[64 * hh:64 * hh + 64, ks], True, True, [iqT, ikT])
                        rl = relur()
                        self.act(rl[:, 0:ncol], ps[:, 0:ncol], AF.Relu, [ps], [rl])
                        if h == 0:
                            self.ts(Isc[:, ks], rl[:, 0:ncol], iwt[:, qb, 0:1], ALU.mult, [rl, iwt], [Isc])
                        else:
                            self.stt(Isc[:, ks], rl[:, 0:ncol], iwt[:, qb, h:h + 1], Isc[:, ks], ALU.mult, ALU.add, [rl, iwt, Isc], [Isc])
                dsl = slice(gq * 128, n)
                self.tt(Isc[:, dsl], Isc[:, dsl], self.negcaus[:, :], ALU.add, [Isc, self.negcaus], [Isc])
                if n > 256:
                    src = Isc
                    m8 = None
                    for r in range(32):
                        m8 = m8r()
                        S.op(S.dve, lambda e, m8=m8, src=src: e.max(out=m8[:, :], in_=src[:, 0:n]), reads=[src], writes=[m8])
                        if r < 31:
                            S.op(S.dve, lambda e, m8=m8, src=src: e.match_replace(out=Wk[:, 0:n], in_to_replace=m8[:, :], in_values=src[:, 0:n],
                                                                                  imm_value=NEGREP), reads=[src, m8], writes=[Wk])
                            src = Wk
                    self.ts(Mq[:, 0:n], Isc[:, 0:n], m8[:, 7:8], ALU.is_ge, [Isc, m8], [Mq])
                    kt = 0
                    while kt <= gq:
                        nb = min(8, gq + 1 - kt)
                        ps = self.psn()
                        psb = ps[:, :].bitcast(BF16)
                        for i in range(nb):
                            S.op(S.pe, lambda e, psb=psb, i=i, kt=kt: e.transpose(out=psb[:, i * 128:(i + 1) * 128],
                                                                                 in_=Mq[:, (kt + i) * 128:(kt + i + 1) * 128], identity=self.identB[:, :]),
                                 reads=[Mq, self.identB], writes=[ps])
                        self.act(MT[:, kt:kt + nb, qsl], psb[:, 0:nb * 128].rearrange("p (i t) -> p i t", i=nb), AF.Copy, [ps], [MT])
                        kt += nb
                else:
                    for kt in range(gq):
                        self.vcopy(MT[:, kt, qsl], self.onesB[:, :], [self.onesB], [MT])
                    self.vcopy(MT[:, gq, qsl], self.causTB[:, :], [self.causTB], [MT])
            nkt = 4 * c + 4
            self.set_ring([5, 6, 7])

            def s_fn(st):
                h, kt = st
                hh = h % 2
                ksl = slice(kt * 128, (kt + 1) * 128)
                cs = slice(128 * max(0, kt - 4 * c), 512)
                Sps = self.psn()
                self.mm(Sps, Sps[:, cs], ckvT[:, ksl], qlatT[:, h, cs], True, False, [ckvT, qlatT])
                self.mm(Sps, Sps[:, cs], krT[32 * hh:32 * hh + 32, ksl], qrT[32 * hh:32 * hh + 32, h // 2, cs], False, True, [krT, qrT])
                return Sps

            def p_fn(st, Sps):
                h, kt = st
                hh = h % 2
                cs = slice(128 * max(0, kt - 4 * c), 512)
                Ops = self.PS[2 * hh]
                Dps = self.PS[2 * hh + 1]
                Pt = Ptr()
                self.act(Pt[:, cs], Sps[:, cs], AF.Exp, [Sps], [Pt], scale=A_SCALE)
                self.tt(Pt[:, cs], Pt[:, cs], MT[:, kt, cs], ALU.mult, [Pt, MT], [Pt])
                self.mm(Ops, Ops[:, cs], ckvTok[:, kt, :], Pt[:, cs], kt == 0, kt == nkt - 1, [ckvTok, Pt])
                self.mm(Dps, Dps[:, cs], self.onesB[:, :], Pt[:, cs], kt == 0, kt == nkt - 1, [self.onesB, Pt])
                if kt == nkt - 1:
                    rd = self.rsring()
                    self.arecip(rd, rd[:, :], Dps, Dps[:, :])
                    ol = olr()
                    self.tt(ol[:, :], Ops[:, :], rd[:, :], ALU.mult, [Ops, rd], [ol])
                    Opair = self.PS[4]
                    self.mm(Opair, Opair[:, :], wuvp[:, h, :], ol[:, :], hh == 0, hh == 1, [wuvp, ol])
                    if hh == 1:
                        self.act(oTaF[:, h // 2, cols], Opair[:, :], AF.Copy, [Opair], [oTaF])

            self.pipeline([(h, kt) for h in range(8) for kt in range(nkt)], s_fn, p_fn)
            self.set_ring(range(8))
        self.pop()
        self.pop()

    def diff(self, l, oTaF, skip=False):
        j = l // 2
        p = f"e{j}_"
        dr = self.dram
        S = self.S
        gcol = l * 8
        if skip:
            self.push()
            oTb = self.sb("oTb", [128, 4, 512], BF16)
            wob = self.sb("wob", [128, 8, 1024], BF16)
            self.load_split(wob, wob[:, :, :], dr[p + "wo"][:, :].rearrange("p (k n) -> p k n", k=8), 8)
            S.op(S.dve, lambda e: e.memset(oTb[:, :, :], 0.0), writes=[oTb])
            for c in range(NCH):
                cols = slice(c * 512, (c + 1) * 512)
                xb = self.xTb[c]
                for m in range(8):
                    ps = self.psn()
                    for k in range(8):
                        rhs = oTaF[:, k, cols] if k < 4 else oTb[:, k - 4, :]
                        self.mm(ps, ps[:, :], wob[:, k, m * 128:(m + 1) * 128], rhs, k == 0, k == 7, [wob, oTaF, oTb])
                    self.tt(self.xT[:, m, cols], self.xT[:, m, cols], ps[:, :], ALU.add, [xb, ps], [xb])
            self.pop()
            return
        lam_init = 0.8 - 0.6 * math.exp(-0.3 * l)
        self.push()
        kbT = self.sb("kbT", [128, 4, T], BF16)
        vbTok = self.sb("vbTok", [128, NTT, 512], BF16)
        lam = self.sb("lam", [1, 256], F32)
        self.load_f(lam, lam[:, :], dr[p + "lam"][:, :])
        sub = self.sb("sub", [128, 1], F32)
        self.load_f(sub, sub[:, :], dr[p + "sub"][:, :])
        pr = self.sb("lpr", [1, 128], F32)
        sm = self.sb("lsm", [1, 2], F32)
        self.tt(pr[:, 0:64], lam[:, 0:64], lam[:, 64:128], ALU.mult, [lam], [pr])
        self.tt(pr[:, 64:128], lam[:, 128:192], lam[:, 192:256], ALU.mult, [lam], [pr])
        S.op(S.dve, lambda e: e.reduce_sum(out=sm[:, 0:1], in_=pr[:, 0:64], axis=AX.X), reads=[pr], writes=[sm])
        S.op(S.dve, lambda e: e.reduce_sum(out=sm[:, 1:2], in_=pr[:, 64:128], axis=AX.X), reads=[pr], writes=[sm])
        self.act(sm[:, :], sm[:, :], AF.Exp, [sm], [sm])
        nl = self.sb("nl", [1, 1], F32)
        self.tt(nl[:, :], sm[:, 1:2], sm[:, 0:1], ALU.subtract, [sm], [nl])
        self.ts(nl[:, :], nl[:, :], -lam_init, ALU.add, [nl], [nl])
        ps = self.psn()
        self.mm(ps, ps[:, 0:1], self.onesF[0:1, :], nl[0:1, 0:1], True, True, [self.onesF, nl])
        neglam = self.sb("neglam", [128, 1], F32)
        self.vcopy(neglam[:, :], ps[:, 0:1], [ps], [neglam])
        self.ts(sub[:, :], sub[:, :], 1.0 - lam_init, ALU.mult, [sub], [sub])
        self.push()
        hT = self.sb("hTk", [128, 8, T], BF16)
        C64 = self.sb("C64", [128, T], F32)
        S64 = self.sb("S64", [128, T], F32)
        for hf in range(2):
            self.load_f(C64, C64[:, hf * 1024:(hf + 1) * 1024], dr["C64"][:, hf * 1024:(hf + 1) * 1024])
            self.load_f(S64, S64[:, hf * 1024:(hf + 1) * 1024], dr["S64"][:, hf * 1024:(hf + 1) * 1024])
        wr = self.ring("wk", 4, [128, 8, 128], BF16)
        wvb = self.sb("wvb", [128, 8, 512], BF16)
        self.load_split(wvb, wvb[:, :, :], dr[p + "vb"][0].rearrange("p (k m) -> p k m", k=8), 4)
        for c in range(NCH):
            self.norm_chunk(c, gcol, hT, c * 512)
        for jj in range(4):
            wk = wr()
            wks = wr()
            self.load_w(wk, wk[:, :, :], dr[p + "kb"][jj].rearrange("p (k m) -> p k m", k=8))
            self.load_w(wks, wks[:, :, :], dr[p + "kbs"][jj].rearrange("p (k m) -> p k m", k=8))
            for c in range(NCH):
                cols = slice(c * 512, (c + 1) * 512)
                pa = self.psn()
                pb = self.psn()
                self.proj(pa, wk, slice(0, 128), 128, hT, cols, 512)
                self.proj(pb, wks, slice(0, 128), 128, hT, cols, 512)
                self.rope_comb(pa, pb, C64, S64, cols, 128, kbT, kbT[:, jj, cols])
        for tt in range(NTT):
            ps = self.psn()
            for k in range(8):
                self.mm(ps, ps[:, :], hT[:, k, tt * 128:(tt + 1) * 128], wvb[:, k, :], k == 0, k == 7, [hT, wvb])
            self.act(vbTok[:, tt, :], ps[:, :], AF.Copy, [ps], [vbTok])
        self.pop()
        self.push()
        hTc = self.sb("hTc", [128, 8, 512], BF16)
        qbT = self.sb("qbT", [128, 4, 512], BF16)
        oTb = self.sb("oTb", [128, 4, 512], BF16)
        wob = self.sb("wob", [128, 8, 1024], BF16)
        self.load_split(wob, wob[:, :, :], dr[p + "wo"][:, :].rearrange("p (k n) -> p k n", k=8), 8)
        wr = self.ring("wq", 4, [128, 8, 128], BF16)
        C64c = self.sb("C64c", [128, 512], F32)
        S64c = self.sb("S64c", [128, 512], F32)
        Ptr = self.ring("Pt", 3, [128, 512], BF16)
        a1r = self.ring("a1", 2, [128, 512], F32)
        dfr = self.ring("df", 2, [128, 512], F32)
        c0 = slice(0, 512)
        for c in range(NCH):
            cols = slice(c * 512, (c + 1) * 512)
            self.norm_chunk(c, gcol, hTc, 0)
            self.load_f(C64c, C64c[:, :], dr["C64"][:, cols])
            self.load_f(S64c, S64c[:, :], dr["S64"][:, cols])
            for jj in range(4):
                wq = wr()
                wqs = wr()
                self.load_w(wq, wq[:, :, :], dr[p + "qb"][jj].rearrange("p (k m) -> p k m", k=8))
                self.load_w(wqs, wqs[:, :, :], dr[p + "qbs"][jj].rearrange("p (k m) -> p k m", k=8))
                pa = self.psn()
                pb = self.psn()
                self.proj(pa, wq, slice(0, 128), 128, hTc, c0, 512)
                self.proj(pb, wqs, slice(0, 128), 128, hTc, c0, 512)
                self.rope_comb(pa, pb, C64c, S64c, c0, 128, qbT, qbT[:, jj, :])
            nkt = 4 * c + 4
            self.set_ring([5, 6, 7])
            am = {}

            def s_fn(st):
                h, m, kt = st
                i = 2 * h + m
                hh = i % 2
                jj = i // 2
                ksl = slice(kt * 128, (kt + 1) * 128)
                cs = slice(128 * max(0, kt - 4 * c), 512)
                Sps = self.psn()
                self.mm(Sps, Sps[:, cs], kbT[64 * hh:64 * hh + 64, jj, ksl], qbT[64 * hh:64 * hh + 64, jj, cs], True, True, [kbT, qbT])
                return Sps

            def p_fn(st, Sps):
                h, m, kt = st
                cs = slice(128 * max(0, kt - 4 * c), 512)
                Ops = self.PS[2 * m]
                Dps = self.PS[2 * m + 1]
                Pt = Ptr()
                self.act(Pt[:, cs], Sps[:, cs], AF.Exp, [Sps], [Pt], scale=0.125)
                if kt >= 4 * c:
                    dg = slice(128 * (kt - 4 * c), 128 * (kt - 4 * c) + 128)
                    self.tt(Pt[:, dg], Pt[:, dg], self.causTB[:, :], ALU.mult, [Pt, self.causTB], [Pt])
                self.mm(Ops, Ops[:, cs], vbTok[:, kt, h * 128:(h + 1) * 128], Pt[:, cs], kt == 0, kt == nkt - 1, [vbTok, Pt])
                self.mm(Dps, Dps[:, cs], self.onesB[:, :], Pt[:, cs], kt == 0, kt == nkt - 1, [self.onesB, Pt])
                if kt == nkt - 1:
                    rd = self.rsring()
                    self.arecip(rd, rd[:, :], Dps, Dps[:, :])
                    a = a1r()
                    self.tt(a[:, :], Ops[:, :], rd[:, :], ALU.mult, [Ops, rd], [a])
                    am[m] = a
                    if m == 1:
                        df = dfr()
                        self.stt(df[:, :], am[1][:, :], neglam[:, 0:1], am[0][:, :], ALU.mult, ALU.add, [am[0], am[1], neglam], [df])
                        self.pnorm(df, sub[:, 0:1], sub, oTb, oTb[:, h, :], 128, ss_bank=self.PS[4])

            self.pipeline([(h, m, kt) for h in range(4) for m in range(2) for kt in range(nkt)], s_fn, p_fn)
            self.set_ring(range(8))
            xb = self.xTb[c]
            for m in range(8):
                ps = self.psn()
                for k in range(8):
                    rhs = oTaF[:, k, cols] if k < 4 else oTb[:, k - 4, :]
                    self.mm(ps, ps[:, :], wob[:, k, m * 128:(m + 1) * 128], rhs, k == 0, k == 7, [wob, oTaF, oTb])
                self.tt(self.xT[:, m, cols], self.xT[:, m, cols], ps[:, :], ALU.add, [xb, ps], [xb])
        self.pop()
        self.pop()


    def nsa(self, l):
        j = l // 2
        p = f"o{j}_"
        dr = self.dram
        S = self.S
        gcol = l * 8
        self.set_ring(range(8))
        self.push()
        ksT = self.sb("ksT", [128, 2, T], BF16)
        kwT = self.sb("kwT", [128, 2, T], BF16)
        vsTok = self.sb("vsTok", [128, NTT, 256], BF16)
        vwTok = self.sb("vwTok", [128, NTT, 256], BF16)
        kcmpT = self.sb("kcmpT", [128, 2, 128], BF16)
        vcmpTok = self.sb("vcmpTok", [128, 256], BF16)
        S.op(S.dve, lambda e: e.memset(kcmpT[:, :, :], 0.0), writes=[kcmpT])
        S.op(S.dve, lambda e: e.memset(vcmpTok[:, :], 0.0), writes=[vcmpTok])
        self.push()
        kcT = self.sb("kcT", [128, 2, T], BF16)
        vcT = self.sb("vcT", [128, 2, T], BF16)
        self.push()
        hT = self.sb("hTk", [128, 8, T], BF16)
        C64 = self.sb("C64", [128, T], F32)
        S64 = self.sb("S64", [128, T], F32)
        for hf in range(2):
            self.load_f(C64, C64[:, hf * 1024:(hf + 1) * 1024], dr["C64"][:, hf * 1024:(hf + 1) * 1024])
            self.load_f(S64, S64[:, hf * 1024:(hf + 1) * 1024], dr["S64"][:, hf * 1024:(hf + 1) * 1024])
        wr = self.ring("wk", 4, [128, 8, 128], BF16)
        for c in range(NCH):
            self.norm_chunk(c, gcol, hT, c * 512)
        for nm, dst, roped in (("kc", kcT, True), ("ks", ksT, True), ("kw", kwT, True), ("vc", vcT, False)):
            for gg in range(2):
                wk = wr()
                self.load_w(wk, wk[:, :, :], dr[p + nm][gg].rearrange("p (k m) -> p k m", k=8))
                if roped:
                    wks = wr()
                    self.load_w(wks, wks[:, :, :], dr[p + nm + "s"][gg].rearrange("p (k m) -> p k m", k=8))
                for c in range(NCH):
                    cols = slice(c * 512, (c + 1) * 512)
                    pa = self.psn()
                    self.proj(pa, wk, slice(0, 128), 128, hT, cols, 512)
                    if roped:
                        pb = self.psn()
                        self.proj(pb, wks, slice(0, 128), 128, hT, cols, 512)
                        self.rope_comb(pa, pb, C64, S64, cols, 128, dst, dst[:, gg, cols])
                    else:
                        self.act(dst[:, gg, cols], pa[:, :], AF.Copy, [pa], [dst])
        wv = self.sb("wv", [128, 8, 256], BF16)
        for nm, dst in (("vs", vsTok), ("vw", vwTok)):
            self.load_split(wv, wv[:, :, :], dr[p + nm][0].rearrange("p (k m) -> p k m", k=8), 2)
            for tt in range(NTT):
                ps = self.psn()
                for k in range(8):
                    self.mm(ps, ps[:, 0:256], hT[:, k, tt * 128:(tt + 1) * 128], wv[:, k, :], k == 0, k == 7, [hT, wv])
                self.act(dst[:, tt, :], ps[:, 0:256], AF.Copy, [ps], [dst])
        self.pop()
        self.push()
        w1 = self.sb("w1", [128, 32, 128], BF16)
        peT = self.sb("peT", [128, 32], BF16)
        w2 = self.sb("w2", [128, 64], BF16)
        cb = self.sb("cb", [128, 1], F32)
        gx = self.ring("gx", 2, [128, 128], F32)
        gu = self.ring("gu", 2, [128, 128], F32)
        Gb = self.ring("Gb", 2, [128, 128], BF16)
        for kv, nm, src in ((0, "k", kcT), (1, "v", vcT)):
            self.load_split(w1, w1[:, :, :], dr[p + "w1" + nm][:, :].rearrange("p (a b) -> p a b", a=32), 4)
            self.load_w(peT, peT[:, :], dr[p + "pe" + nm][:, :])
            self.load_w(w2, w2[:, :], dr[p + "w2" + nm][:, :])
            ps = self.psn()
            for pp in range(32):
                self.mm(ps, ps[:, 0:1], w1[0:64, pp, :], peT[0:64, pp:pp + 1], pp == 0, pp == 31, [w1, peT])
            self.vcopy(cb[:, :], ps[:, 0:1], [ps], [cb])
            for g in range(4):
                hf = g % 2
                gg = g // 2
                ps = self.psn()
                for pp in range(32):
                    self.mm(ps, ps[:, 0:127], w1[64 * hf:64 * hf + 64, pp, :], src[64 * hf:64 * hf + 64, gg, pp:pp + 2017:16],
                            pp == 0, pp == 31, [w1, src])
                x_ = gx()
                u_ = gu()
                G = Gb()
                self.ts(x_[:, 0:127], ps[:, 0:127], cb[:, 0:1], ALU.add, [ps, cb], [x_])
                self.tt(u_[:, 0:127], x_[:, 0:127], x_[:, 0:127], ALU.mult, [x_], [u_])
                self.ts(u_[:, 0:127], u_[:, 0:127], 0.044715, ALU.mult, [u_], [u_], s2=1.0, op1=ALU.add)
                self.tt(u_[:, 0:127], u_[:, 0:127], x_[:, 0:127], ALU.mult, [u_, x_], [u_])
                self.act(u_[:, 0:127], u_[:, 0:127], AF.Sigmoid, [u_], [u_], scale=1.5957691216057308)
                self.tt(G[:, 0:127], u_[:, 0:127], x_[:, 0:127], ALU.mult, [u_, x_], [G])
                ps2 = self.psn()
                if kv == 0:
                    self.mm(ps2, ps2[0:64, 0:127], w2[:, :], G[:, 0:127], True, True, [w2, G])
                    self.act(kcmpT[64 * hf:64 * hf + 64, gg, 0:127], ps2[0:64, 0:127], AF.Copy, [ps2], [kcmpT])
                else:
                    self.mm(ps2, ps2[0:127, 0:64], G[:, 0:127], w2[:, :], True, True, [w2, G])
                    self.act(vcmpTok[0:127, g * 64:(g + 1) * 64], ps2[0:127, 0:64], AF.Copy, [ps2], [vcmpTok])
        self.pop()
        self.pop()
        self.push()
        maskc = self.sb("maskc", [128, T], BF16)
        SELADD = self.sb("SELADD", [128, 16, 32], F32)
        EXPB = self.sb("EXPB", [32, 16, 128], BF16)
        SELB = self.sb("SELB", [48, 48, 64], BF16)
        ovl = self.sb("ovl", [128, 32], BF16)
        antiTB = self.sb("antiTB", [128, 128], BF16)
        self.load_split(maskc, maskc[:, :].rearrange("p (a b) -> p a b", a=2), dr["maskcmpT"][:, :].rearrange("p (a b) -> p a b", a=2), 2)
        self.load_f(SELADD, SELADD[:, :, :], dr["SELADD"][:, :].rearrange("p (a b) -> p a b", a=16))
        self.load_w(EXPB, EXPB[:, :, :], dr["EXPB"][:, :].rearrange("p (a b) -> p a b", a=16))
        self.load_w(SELB, SELB[:, :, :], dr["SELB"][:, :].rearrange("p (a b) -> p a b", a=48))
        self.load_w(ovl, ovl[:, :], dr["ovl"][:, :])
        self.load_w(antiTB, antiTB[:, :], dr["antiT"][:, :])
        hTc = self.sb("hTc", [128, 8, 512], BF16)
        qT = self.sb("qT", [128, 8, 512], BF16)
        MS = self.sb("MS", [128, NTT, 512], BF16)
        oT = self.sb("oT", [128, 8, 512], BF16)
        wr = self.ring("wq", 3, [128, 8, 128], BF16)
        C64c = self.sb("C64c", [128, 512], F32)
        S64c = self.sb("S64c", [128, 512], F32)
        gs = self.sb("gs", [48, 512], F32)
        ghi = self.sb("ghi", [48, 512], BF16)
        glo = self.sb("glo", [48, 512], BF16)
        wg = self.sb("wg", [128, 8, 48], BF16)
        self.load_w(wg, wg[:, :, :], dr[p + "g"][0].rearrange("p (k m) -> p k m", k=8))
        Ptr = self.ring("Pt", 2, [128, 512], BF16)
        Er = Ptr
        ocmp = self.sb("ocmp", [64, 4, 512], BF16)
        accr = self.ring("acc", 2, [64, 512], F32)
        cfr = self.ring("cf", 1, [64, 512], F32)
        tmr = self.ring("tm", 1, [64, 512], F32)
        imp2r = self.ring("imp2", 2, [128, 32], F32)
        w2r = self.ring("w2r", 2, [128, 32], F32)
        m8r = self.ring("m8", 4, [128, 8], F32)
        selr = self.ring("sel", 2, [128, 32], BF16)
        selT = self.sb("selT", [32, 512], BF16)
        impS = self.sb("impS", [32, 512], F32)
        c0 = slice(0, 512)
        for c in range(NCH):
            cols = slice(c * 512, (c + 1) * 512)
            self.set_ring(range(8))
            self.norm_chunk(c, gcol, hTc, 0)
            self.load_f(C64c, C64c[:, :], dr["C64"][:, cols])
            self.load_f(S64c, S64c[:, :], dr["S64"][:, cols])
            for qi in range(8):
                wq = wr()
                wqs = wr()
                self.load_w(wq, wq[:, :, :], dr[p + "q"][qi].rearrange("p (k m) -> p k m", k=8))
                self.load_w(wqs, wqs[:, :, :], dr[p + "qs"][qi].rearrange("p (k m) -> p k m", k=8))
                pa = self.psn()
                pb = self.psn()
                self.proj(pa, wq, slice(0, 128), 128, hTc, c0, 512)
                self.proj(pb, wqs, slice(0, 128), 128, hTc, c0, 512)
                self.rope_comb(pa, pb, C64c, S64c, c0, 128, qT, qT[:, qi, :])
            ps = self.psn()
            self.proj(ps, wg, slice(0, 48), 48, hTc, c0, 512)
            self.act(gs[:, :], ps[0:48, :], AF.Sigmoid, [ps], [gs])
            self.vcopy(ghi[:, :], gs[:, :], [gs], [ghi])
            self.tt(glo[:, :], gs[:, :], ghi[:, :], ALU.subtract, [gs, ghi], [glo])
            self.set_ring([5, 6, 7])
            ncmp = 32 * c + 31
            nkt = 4 * c + 4
            for g in range(4):
                hf = g % 2
                gg = g // 2
                hs = slice(64 * hf, 64 * hf + 64)
                imp = self.PS[0]
                for i in range(4):
                    qi = gg * 4 + i
                    Sps = self.psn()
                    self.mm(Sps, Sps[0:ncmp, :], kcmpT[hs, gg, 0:ncmp], qT[hs, qi, :], True, True, [kcmpT, qT])
                    E = Er()
                    self.act(E[0:ncmp, :], Sps[0:ncmp, :], AF.Exp, [Sps], [E], scale=0.125)
                    self.tt(E[0:ncmp, :], E[0:ncmp, :], maskc[0:ncmp, cols], ALU.mult, [E, maskc], [E])
                    Dps = self.psn()
                    self.mm(Dps, Dps[:, :], self.onesB[0:ncmp, :], E[0:ncmp, :], True, True, [self.onesB, E])
                    rd = self.rsring()
                    self.arecip(rd, rd[:, :], Dps, Dps[:, :], bias=1e-30)
                    Pn = Ptr()
                    self.tt(Pn[0:ncmp, :], E[0:ncmp, :], rd[0:ncmp, :], ALU.mult, [E, rd], [Pn])
                    Ops = self.psn()
                    self.mm(Ops, Ops[0:64, :], vcmpTok[0:ncmp, g * 64:(g + 1) * 64], Pn[0:ncmp, :], True, True, [vcmpTok, Pn])
                    self.act(ocmp[:, i, :], Ops[0:64, :], AF.Copy, [Ops], [ocmp])
                    self.mm(imp, imp[0:32, :], ovl[0:ncmp, :], Pn[0:ncmp, :], i == 0, i == 3, [Pn, ovl])
                self.act(impS[:, :], imp[0:32, :], AF.Copy, [imp], [impS])
                for qb in range(4):
                    gq = 4 * c + qb
                    pti = self.psn()
                    S.op(S.pe, lambda e, pti=pti, qb=qb: e.transpose(out=pti[:, 0:32], in_=impS[0:32, qb * 128:(qb + 1) * 128],
                                                                    identity=self.identF[0:32, 0:32]), reads=[impS, self.identF], writes=[pti])
                    i2 = imp2r()
                    self.tt(i2[:, :], pti[:, 0:32], SELADD[:, gq, :], ALU.add, [pti, SELADD], [i2])
                    ma = m8r()
                    S.op(S.dve, lambda e, ma=ma, i2=i2: e.max(out=ma[:, :], in_=i2[:, :]), reads=[i2], writes=[ma])
                    w2_ = w2r()
                    S.op(S.dve, lambda e, ma=ma, i2=i2, w2_=w2_: e.match_replace(out=w2_[:, :], in_to_replace=ma[:, :], in_values=i2[:, :],
                                                                                 imm_value=NEGREP), reads=[i2, ma], writes=[w2_])
                    mb = m8r()
                    S.op(S.dve, lambda e, mb=mb, w2_=w2_: e.max(out=mb[:, :], in_=w2_[:, :]), reads=[w2_], writes=[mb])
                    sel = selr()
                    self.ts(sel[:, :], i2[:, :], mb[:, 7:8], ALU.is_ge, [i2, mb], [sel])
                    pst = self.psn()
                    pstb = pst[:, :].bitcast(BF16)
                    S.op(S.pe, lambda e, pstb=pstb, sel=sel: e.transpose(out=pstb[0:32, 0:128], in_=sel[:, :], identity=self.identB[:, :]),
                         reads=[sel, self.identB], writes=[pst])
                    self.act(selT[:, qb * 128:(qb + 1) * 128], pstb[0:32, 0:128], AF.Copy, [pst], [selT])
                for kt in range(nkt):
                    pm = self.psn()
                    self.mm(pm, pm[:, :], EXPB[:, kt, :], selT[:, :], True, True, [EXPB, selT])
                    if kt >= 4 * c:
                        self.tt(MS[:, kt, :], pm[:, :], self.CM[:, kt - 4 * c, :], ALU.mult, [pm, self.CM], [MS])
                    else:
                        self.act(MS[:, kt, :], pm[:, :], AF.Copy, [pm], [MS])
                self.set_ring([5, 6, 7])
                accs = {}

                def colrange(br, kt):
                    if kt >= 4 * c:
                        return slice(128 * (kt - 4 * c), 512)
                    if br == 2:
                        return slice(0, 128 * (kt - (4 * c - 4)) + 128)
                    return slice(0, 512)

                def s_fn(st):
                    i, br, n_, kt, last = st
                    qi = gg * 4 + i
                    kT_ = ksT if br == 1 else kwT
                    ksl = slice(kt * 128, (kt + 1) * 128)
                    cs = colrange(br, kt)
                    Sps = self.psn()
                    self.mm(Sps, Sps[:, cs], kT_[hs, gg, ksl], qT[hs, qi, cs], True, True, [kT_, qT])
                    return Sps

                def p_fn(st, Sps):
                    i, br, n_, kt, last = st
                    qi = gg * 4 + i
                    hd = g * 4 + i
                    vTok = vsTok if br == 1 else vwTok
                    Ops = self.PS[1 + 2 * (br - 1)]
                    Dps = self.PS[2 + 2 * (br - 1)]
                    cs = colrange(br, kt)
                    Pt = Ptr()
                    self.act(Pt[:, cs], Sps[:, cs], AF.Exp, [Sps], [Pt], scale=0.125)
                    if br == 1 and n_ == 0:
                        acc = accr()
                        accs[i] = acc
                        gps = self.PS[0]
                        r = hd * 3 + 0
                        self.mm(gps, gps[0:64, :], SELB[:, r, :], ghi[:, :], True, False, [SELB, ghi])
                        self.mm(gps, gps[0:64, :], SELB[:, r, :], glo[:, :], False, True, [SELB, glo])
                        self.tt(acc[:, :], ocmp[:, i, :], gps[0:64, :], ALU.mult, [ocmp, gps], [acc])
                    acc = accs[i]
                    if br == 1:
                        self.tt(Pt[:, cs], Pt[:, cs], MS[:, kt, cs], ALU.mult, [Pt, MS], [Pt])
                    elif kt >= 4 * c:
                        dg = slice(128 * (kt - 4 * c), 128 * (kt - 4 * c) + 128)
                        self.tt(Pt[:, dg], Pt[:, dg], self.causTB[:, :], ALU.mult, [Pt, self.causTB], [Pt])
                    else:
                        dg = slice(128 * (kt - (4 * c - 4)), 128 * (kt - (4 * c - 4)) + 128)
                        self.tt(Pt[:, dg], Pt[:, dg], antiTB[:, :], ALU.mult, [Pt, antiTB], [Pt])
                    self.mm(Ops, Ops[0:64, cs], vTok[:, kt, g * 64:(g + 1) * 64], Pt[:, cs], n_ == 0, last, [vTok, Pt])
                    self.mm(Dps, Dps[0:64, cs], self.onesB[:, 0:64], Pt[:, cs], n_ == 0, last, [self.onesB, Pt])
                    if last:
                        gps = self.PS[0]
                        r = hd * 3 + br
                        self.mm(gps, gps[0:64, :], SELB[:, r, :], ghi[:, :], True, False, [SELB, ghi])
                        self.mm(gps, gps[0:64, :], SELB[:, r, :], glo[:, :], False, True, [SELB, glo])
                        cf = cfr()
                        self.arecip(cf, cf[:, :], Dps, Dps[0:64, :])
                        self.tt(cf[:, :], cf[:, :], gps[0:64, :], ALU.mult, [cf, gps], [cf])
                        tm = tmr()
                        self.tt(tm[:, :], Ops[0:64, :], cf[:, :], ALU.mult, [Ops, cf], [tm])
                        self.tt(acc[:, :], acc[:, :], tm[:, :], ALU.add, [acc, tm], [acc])
                        if br == 2:
                            self.act(oT[hs, qi, :], acc[:, :], AF.Copy, [acc], [oT])

                steps = []
                for i in range(4):
                    for br in (1, 2):
                        kts = list(range(nkt)) if br == 1 else (list(range(4 * c - 1, max(0, 4 * c - 4) - 1, -1)) + list(range(4 * c, nkt)))
                        for n_, kt in enumerate(kts):
                            steps.append((i, br, n_, kt, n_ == len(kts) - 1))
                self.pipeline(steps, s_fn, p_fn)
                self.set_ring([5, 6, 7])
            self.set_ring(range(8))
            xb = self.xTb[c]
            for m in range(8):
                wo = wr()
                self.load_w(wo, wo[:, :, :], dr[p + "wo"][m].rearrange("p (k m) -> p k m", k=8))
                ps = self.psn()
                self.proj(ps, wo, slice(0, 128), 128, oT, c0, 512)
                self.tt(self.xT[:, m, cols], self.xT[:, m, cols], ps[:, :], ALU.add, [xb, ps], [xb])
        self.pop()
        self.pop()

    def build(self):
        nc = self.nc
        S = self.S
        dr = self.dram
        self.xT = self.sb("xT", [128, 8, T], F32)
        self.xTb = [Buf(self.xT.t) for _ in range(NCH)]
        self.gains = self.sb("gains", [128, 72], F32)
        self.identF = self.sb("identF", [128, 128], F32)
        self.identB = self.sb("identB", [128, 128], BF16)
        self.onesB = self.sb("onesB", [128, 128], BF16)
        self.onesF = self.sb("onesF", [128, 128], F32)
        self.causTB = self.sb("causTB", [128, 128], BF16)
        self.negcaus = self.sb("negcaus", [128, 128], F32)
        self.CM = self.sb("CM", [128, 4, 512], BF16)
        self.outbuf = Buf(None)
        self.PS = []
        for i in range(8):
            t = self.stk[0].enter_context(nc.psum_tensor(f"ps{i}", [128, 512], F32))
            self.PS.append(Buf(t))
        self.ps_i = 0
        self.ring_banks = list(range(8))
        self.sqring = self.ring("sq", 2, [128, 512], BF16)
        self.rsring = self.ring("rs", 2, [128, 512], F32)
        self.t1ring = self.ring("t1", 2, [128, 512], F32)
        self.load_f(self.gains, self.gains[:, :], dr["gains"][:, :])
        self.load_f(self.identF, self.identF[:, :], dr["identF"][:, :])
        self.load_f(self.negcaus, self.negcaus[:, :], dr["negcaus"][:, :])
        self.load_w(self.identB, self.identB[:, :], dr["identF"][:, :])
        self.load_w(self.causTB, self.causTB[:, :], dr["causT"][:, :])
        self.load_w(self.CM, self.CM[:, :, :], dr["CM"][:, :].rearrange("p (i q) -> p i q", i=4))
        S.op(S.dve, lambda e: e.memset(self.onesB[:, :], 1.0), writes=[self.onesB])
        S.op(S.dve, lambda e: e.memset(self.onesF[:, :], 1.0), writes=[self.onesF])
        for s in range(self.nseq):
            self.set_ring(range(8))
            self.push()
            self.xinring = self.ring("xin", 2, [128, 1024], F32)
            self.load_x(s)
            self.pop()
            for l in self.layers:
                if "mix" in self.parts:
                    if l % 2 == 0:
                        self.push()
                        oTaF = self.sb("oTaF", [128, 4, T], BF16)
                        if "dsa" in self.parts or "all" in self.parts:
                            self.dsa(l, oTaF)
                        else:
                            self.S.op(self.S.dve, lambda e: e.memset(oTaF[:, :, :], 0.0), writes=[oTaF])
                        self.diff(l, oTaF, skip=not ("diff" in self.parts or "all" in self.parts))
                        self.pop()
                    else:
                        self.nsa(l)
                if "ffn" in self.parts:
                    self.ffn(l)
            if self.final:
                self.set_ring(range(8))
                self.push()
                self.xinring = self.ring("xin", 2, [128, 1024], F32)
                self.store_out(s)
                self.pop()
        S.barrier()


def build_and_run(inputs, nseq_per_core, n_cores, layers, parts, final=True):
    consts = host_consts()
    wts = prep_weights(inputs)
    wshapes = {k: v.shape for k, v in wts.items()}
    cshapes = {k: v.shape for k, v in consts.items()}
    prog = Prog(nseq_per_core, layers, wshapes, cshapes, parts=parts, final=final)
    x = np.ascontiguousarray(inputs["x"], dtype=np.float32)
    in_maps = []
    for ci in range(n_cores):
        m = {"x": x[ci * nseq_per_core:(ci + 1) * nseq_per_core]}
        m.update(wts)
        m.update(consts)
        in_maps.append(m)
    res = run_bass_kernel_spmd(prog.nc, in_maps, core_ids=list(range(n_cores)))
    return np.concatenate([r["out"] for r in res.results], axis=0)


def kernel(**inputs):
    inputs = {k: np.asarray(v) for k, v in inputs.items()}
    return build_and_run(inputs, 2, N_CORES, list(range(DEPTH)), ("mix", "all", "ffn"), final=True).astype(np.float32)
```

```python
import math
from contextlib import ExitStack
import numpy as np
import ml_dtypes
import concourse.bass as bass
import concourse.mybir as mybir
from concourse.bass_utils import run_bass_kernel_spmd

F32 = mybir.dt.float32
BF16 = mybir.dt.bfloat16
AF = mybir.ActivationFunctionType
ALU = mybir.AluOpType
AX = mybir.AxisListType

T = 2048
D = 1024
NTT = 16
NCH = 4
DFF = 2816
NFF = 22
EPS = 1e-6
NEG = -1.0e30
NEGREP = -3.0e38
A_SCALE = 96 ** -0.5
N_CORES = 8
DEPTH = 4


class Src:
    def __init__(self, sem, name):
        self.sem = sem
        self.cnt = 0
        self.name = name


class Eng:
    def __init__(self, h, src, name, is_pe=False):
        self.h = h
        self.src = src
        self.name = name
        self.seen = {}
        self.is_pe = is_pe


class Buf:
    __slots__ = ("t", "w", "r")

    def __init__(self, t):
        self.t = t
        self.w = None
        self.r = {}

    def __getitem__(self, k):
        return self.t[k]


class Sched:
    ROT = 30000

    def __init__(self, nc, n_dma_slots=32):
        self.nc = nc
        self._stack = ExitStack()
        self._n = 0
        self.pe = Eng(nc.tensor, self._mk("pe"), "pe", is_pe=True)
        self.act = Eng(nc.scalar, self._mk("act"), "act")
        self.dve = Eng(nc.vector, self._mk("dve"), "dve")
        self.pool = Eng(nc.gpsimd, self._mk("pool"), "pool")
        self.sp = Eng(nc.sync, self._mk("sp"), "sp")
        self.engs = [self.pe, self.act, self.dve, self.pool, self.sp]
        self.slots = [self._mk(f"dma{i}") for i in range(n_dma_slots)]
        self.slot_i = 0
        self.slot_sw = 0
        self.slot_hw = 0
        self.n_ins = 0
        self.all_srcs = [e.src for e in self.engs] + list(self.slots)

    def _mk(self, name):
        self._n += 1
        s = self._stack.enter_context(self.nc.semaphore(f"s{self._n}_{name}"))
        return Src(s, name)

    def close(self):
        self._stack.close()

    def _wait(self, eng, s, c):
        if eng.seen.get(s, 0) >= c:
            return
        eng.h.wait_ge(s.sem, c)
        eng.seen[s] = c
        self.n_ins += 1

    def _deps(self, eng, reads, writes):
        deps = {}
        for b in reads:
            if b.w is not None:
                s, c = b.w
                if deps.get(s, 0) < c:
                    deps[s] = c
        for b in writes:
            if b.w is not None:
                s, c = b.w
                if deps.get(s, 0) < c:
                    deps[s] = c
            for s, c in b.r.items():
                if deps.get(s, 0) < c:
                    deps[s] = c
        for s, c in deps.items():
            if eng.is_pe and s is eng.src:
                continue
            self._wait(eng, s, c)

    def op(self, eng, fn, reads=(), writes=()):
        if eng.src.cnt >= self.ROT:
            new = self._mk(eng.name)
            self.all_srcs.append(new)
            eng.src = new
        self._deps(eng, reads, writes)
        ins = fn(eng.h)
        eng.src.cnt += 1
        ins.then_inc(eng.src.sem, 1)
        c = eng.src.cnt
        s = eng.src
        for b in reads:
            b.r[s] = c
        for b in writes:
            b.w = (s, c)
            b.r = {}
        self.n_ins += 1
        return ins

    def dma(self, eng, out, in_, reads=(), writes=(), **kw):
        half = len(self.slots) // 2
        if eng is self.pool:
            idx = self.slot_sw % half
            self.slot_sw += 1
        else:
            idx = half + (self.slot_hw % half)
            self.slot_hw += 1
        slot = self.slots[idx]
        if slot.cnt >= self.ROT:
            new = self._mk("dma")
            self.all_srcs.append(new)
            self.slots[idx] = new
            self._wait(eng, slot, slot.cnt)
            slot = new
        self._deps(eng, reads, writes)
        if slot.cnt > 0:
            self._wait(eng, slot, slot.cnt)
        ins = eng.h.dma_start(out=out, in_=in_, **kw)
        slot.cnt += 16
        ins.then_inc(slot.sem, 16)
        c = slot.cnt
        for b in reads:
            b.r[slot] = c
        for b in writes:
            b.w = (slot, c)
            b.r = {}
        self.n_ins += 1
        return ins

    def barrier(self):
        for e in self.engs:
            for s in self.all_srcs:
                if s.cnt > 0:
                    if e.is_pe and s is e.src:
                        continue
                    self._wait(e, s, s.cnt)


def chunkify(W, mc):
    K, n = W.shape
    kk = K // 128
    nch = n // mc
    return np.ascontiguousarray(W.reshape(kk, 128, nch, mc).transpose(2, 1, 0, 3).reshape(nch, 128, kk * mc))


def rope_swap_cols(ncols, d):
    idx = np.arange(ncols)
    h = idx // d
    i = idx % d
    return h * d + (i + d // 2) % d


def rope_tables(d, nrep):
    inv = (10000.0 ** (-np.arange(0, d, 2, dtype=np.float32) / np.float32(d))).astype(np.float32)
    pos = np.arange(T, dtype=np.float32)
    ang = (pos[:, None] * inv[None, :]).astype(np.float32)
    cos = np.cos(ang).astype(np.float32).T
    sin = np.sin(ang).astype(np.float32).T
    C = np.concatenate([cos, cos], 0)
    Sg = np.concatenate([-sin, sin], 0)
    return (np.ascontiguousarray(np.tile(C, (nrep, 1))), np.ascontiguousarray(np.tile(Sg, (nrep, 1))))


EVEN_SPLITS = [512, 256, 128, 32, 512, 64, 8, 512, 512, 512]
ODD_SPLITS = [1024] + [256] * 6 + [48]


def host_consts():
    c = {}
    c["identF"] = np.eye(128, dtype=np.float32)
    k = np.arange(128)[:, None]
    q = np.arange(128)[None, :]
    c["causT"] = (k <= q).astype(np.float32)
    c["antiT"] = (k > q).astype(np.float32)
    c["negcaus"] = np.where(q <= k, 0.0, NEG).astype(np.float32)
    qq = np.arange(512)[None, None, :]
    ii = np.arange(4)[None, :, None]
    kk = np.arange(128)[:, None, None]
    c["CM"] = ((128 * ii + kk) <= qq).astype(np.float32).reshape(128, 4 * 512)
    c["WM"] = (qq < (128 * ii + kk)).astype(np.float32).reshape(128, 4 * 512)
    n = np.arange(128)[:, None]
    t = np.arange(T)[None, :]
    c["maskcmpT"] = ((16 * n + 31 <= t) & (n < 127)).astype(np.float32)
    jb = np.arange(32)[None, :]
    c["ovl"] = ((16 * n < 64 * jb + 64) & (16 * n + 32 > 64 * jb) & (n < 127)).astype(np.float32)
    sa = np.zeros((128, 16, 32), np.float32)
    for gq in range(16):
        tt_ = 128 * gq + np.arange(128)
        cur = (tt_ // 64)[:, None]
        j2 = np.arange(32)[None, :]
        v = np.zeros((128, 32), np.float64)
        v = np.where(j2 > cur, NEG * (1.0 + j2 / 64.0), v)
        v = v + np.where(j2 == cur, 1.0e30, 0.0) + np.where(j2 == cur - 1, 2.0e30, 0.0)
        v = v + np.where((j2 == 0), 4.0e30, 0.0)
        sa[:, gq, :] = v.astype(np.float32)
    c["SELADD"] = sa.reshape(128, 16 * 32)
    eb = np.zeros((32, 16, 128), np.float32)
    for kt in range(16):
        eb[2 * kt, kt, 0:64] = 1.0
        eb[2 * kt + 1, kt, 64:128] = 1.0
    c["EXPB"] = eb.reshape(32, 16 * 128)
    sb_ = np.zeros((48, 48, 64), np.float32)
    for r in range(48):
        sb_[r, r, :] = 1.0
    c["SELB"] = sb_.reshape(48, 48 * 64)
    C64, S64 = rope_tables(64, 2)
    C32, S32 = rope_tables(32, 2)
    c["C64"], c["S64"], c["C32"], c["S32"] = C64, S64, C32, S32
    return c


def prep_weights(inp):
    w = {}
    g = np.zeros((128, 72), np.float32)
    for l in range(DEPTH):
        g[:, l * 8:(l + 1) * 8] = inp["norm_mix"][l].reshape(8, 128).T
        g[:, 32 + l * 8:32 + (l + 1) * 8] = inp["norm_ffn"][l].reshape(8, 128).T
    g[:, 64:72] = inp["norm_final"].reshape(8, 128).T
    w["gains"] = g
    for l in range(DEPTH):
        w[f"f{l}_wg"] = chunkify(inp["ffn_w_gate"][l], 128)
        w[f"f{l}_wu"] = chunkify(inp["ffn_w_up"][l], 128)
        Wd = inp["ffn_w_down"][l]
        w[f"f{l}_wd"] = np.ascontiguousarray(Wd.reshape(NFF, 128, 8, 128).transpose(2, 1, 0, 3).reshape(8, 128, NFF * 128))
    for j in range(2):
        W = inp["ev_w_in"][j]
        offs = np.cumsum([0] + EVEN_SPLITS)
        qn, qr, ckv, kr, iq, ik, iw, qb, kb, vb = [W[:, offs[i]:offs[i + 1]] for i in range(10)]
        p = f"e{j}_"
        w[p + "ckv"] = chunkify(ckv, 128)
        kr2 = np.concatenate([kr, kr], 1)
        w[p + "kr"] = chunkify(kr2, 64)
        w[p + "krs"] = chunkify(kr2[:, rope_swap_cols(64, 32)], 64)
        ik2 = np.concatenate([ik, ik], 1)
        w[p + "ik"] = chunkify(ik2, 128)
        w[p + "iks"] = chunkify(ik2[:, rope_swap_cols(128, 64)], 128)
        w[p + "qn"] = chunkify(qn, 128)
        w[p + "qr"] = chunkify(qr, 64)
        w[p + "qrs"] = chunkify(qr[:, rope_swap_cols(256, 32)], 64)
        w[p + "iq"] = chunkify(iq, 128)
        w[p + "iqs"] = chunkify(iq[:, rope_swap_cols(512, 64)], 128)
        w[p + "iw"] = chunkify(iw, 8)
        w[p + "kb"] = chunkify(kb, 128)
        w[p + "kbs"] = chunkify(kb[:, rope_swap_cols(512, 64)], 128)
        w[p + "qb"] = chunkify(qb, 128)
        w[p + "qbs"] = chunkify(qb[:, rope_swap_cols(512, 64)], 128)
        w[p + "vb"] = chunkify(vb, 512)
        uk = inp["ev_w_uk"][j]
        w[p + "uk"] = np.ascontiguousarray(uk.reshape(4, 2, 64, 128).transpose(1, 2, 0, 3).reshape(128, 4 * 128))
        uv = inp["ev_w_uv"][j]
        uvp = np.zeros((128, 8, 128), np.float32)
        for h in range(8):
            uvp[:, h, (h % 2) * 64:(h % 2) * 64 + 64] = uv[h]
        w[p + "uvp"] = uvp.reshape(128, 8 * 128)
        wo = inp["ev_w_out"][j]
        w[p + "wo"] = np.ascontiguousarray(wo.reshape(8, 128, 1024).transpose(1, 0, 2).reshape(128, 8 * 1024))
        w[p + "kvg"] = np.ascontiguousarray(inp["ev_kv_gain"][j].reshape(128, 1))
        w[p + "sub"] = np.ascontiguousarray(inp["ev_subln"][j].reshape(128, 1))
        w[p + "lam"] = np.ascontiguousarray(inp["ev_lambda"][j].reshape(1, 256))
    for j in range(2):
        W = inp["od_w_in"][j]
        offs = np.cumsum([0] + ODD_SPLITS)
        qc, kc, vc, ks, vs, kw, vw, gc = [W[:, offs[i]:offs[i + 1]] for i in range(8)]
        p = f"o{j}_"
        perm = []
        for gg in range(2):
            for i in range(4):
                for hf in range(2):
                    hd = (2 * gg + hf) * 4 + i
                    perm.extend(range(hd * 64, hd * 64 + 64))
        perm = np.array(perm)
        qp = qc[:, perm]
        w[p + "q"] = chunkify(qp, 128)
        w[p + "qs"] = chunkify(qp[:, rope_swap_cols(1024, 64)], 128)
        for nm, a in (("kc", kc), ("ks", ks), ("kw", kw)):
            w[p + nm] = chunkify(a, 128)
            w[p + nm + "s"] = chunkify(a[:, rope_swap_cols(256, 64)], 128)
        w[p + "vc"] = chunkify(vc, 128)
        w[p + "vs"] = chunkify(vs, 256)
        w[p + "vw"] = chunkify(vw, 256)
        w[p + "g"] = chunkify(gc, 48)
        for kv, nm in ((0, "k"), (1, "v")):
            w1 = inp["od_cmp_w1"][j][kv]
            a = w1.reshape(32, 64, 128).transpose(1, 0, 2).reshape(64, 32 * 128)
            w[p + "w1" + nm] = np.ascontiguousarray(np.concatenate([a, a], 0))
            pe = inp["od_cmp_pe"][j][kv]
            w[p + "pe" + nm] = np.ascontiguousarray(np.concatenate([pe.T, pe.T], 0))
            w[p + "w2" + nm] = np.ascontiguousarray(inp["od_cmp_w2"][j][kv])
        wo = inp["od_w_out"][j]
        w[p + "wo"] = chunkify(wo[perm, :], 128)
    return w


class Prog:
    def __init__(self, nseq, layers, wshapes, cshapes, parts=("mix", "ffn"), final=True):
        self.nseq = nseq
        nc = self.nc = bass.Bass("TRN2", target_bir_lowering=False)
        self.S = Sched(nc)
        self.stk = [ExitStack()]
        self._u = 0
        self.dram = {}
        self.x = nc.dram_tensor("x", [nseq, T, D], F32, kind="ExternalInput").ap()
        self.out = nc.dram_tensor("out", [nseq, T, D], F32, kind="ExternalOutput").ap()
        for k, shp in list(wshapes.items()) + list(cshapes.items()):
            self.dram[k] = nc.dram_tensor(k, list(shp), F32, kind="ExternalInput").ap()
        self.layers = layers
        self.parts = parts
        self.final = final
        self.build()
        self.S.close()

    def sb(self, name, shape, dt):
        self._u += 1
        t = self.stk[-1].enter_context(self.nc.sbuf_tensor(f"{name}_{self._u}", list(shape), dt))
        return Buf(t)

    def push(self):
        self.stk.append(ExitStack())

    def pop(self):
        self.S.barrier()
        self.stk.pop().close()

    def ring(self, name, n, shape, dt):
        bufs = [self.sb(f"{name}{i}", shape, dt) for i in range(n)]
        state = {"i": 0}

        def nxt():
            b = bufs[state["i"] % n]
            state["i"] += 1
            return b
        return nxt

    def mm(self, ps, out_ap, lhsT_ap, rhs_ap, start, stop, reads):
        self.S.op(self.S.pe, lambda e: e.matmul(out_ap, lhsT=lhsT_ap, rhs=rhs_ap, start=start, stop=stop),
                  reads=reads, writes=[ps])

    def act(self, out_ap, in_ap, func, reads, writes, **kw):
        self.S.op(self.S.act, lambda e: e.activation(out=out_ap, in_=in_ap, func=func, **kw), reads=reads, writes=writes)

    def tt(self, out_ap, a_ap, b_ap, op, reads, writes):
        self.S.op(self.S.dve, lambda e: e.tensor_tensor(out=out_ap, in0=a_ap, in1=b_ap, op=op), reads=reads, writes=writes)

    def ts(self, out_ap, a_ap, s1, op0, reads, writes, s2=None, op1=None):
        if op1 is None:
            self.S.op(self.S.dve, lambda e: e.tensor_scalar(out=out_ap, in0=a_ap, scalar1=s1, scalar2=None, op0=op0),
                      reads=reads, writes=writes)
        else:
            self.S.op(self.S.dve, lambda e: e.tensor_scalar(out=out_ap, in0=a_ap, scalar1=s1, scalar2=s2, op0=op0, op1=op1),
                      reads=reads, writes=writes)

    def stt(self, out_ap, a_ap, sc, b_ap, op0, op1, reads, writes):
        self.S.op(self.S.dve, lambda e: e.scalar_tensor_tensor(out=out_ap, in0=a_ap, scalar=sc, in1=b_ap, op0=op0, op1=op1),
                  reads=reads, writes=writes)

    def vcopy(self, out_ap, in_ap, reads, writes):
        self.S.op(self.S.dve, lambda e: e.tensor_copy(out=out_ap, in_=in_ap), reads=reads, writes=writes)

    def recip(self, out_ap, in_ap, reads, writes):
        self.S.op(self.S.dve, lambda e: e.reciprocal(out=out_ap, in_=in_ap), reads=reads, writes=writes)

    def load_w(self, dst, dst_ap, src_ap):
        self.S.dma(self.S.pool, dst_ap, src_ap, writes=[dst])

    def load_f(self, dst, dst_ap, src_ap):
        self.S.dma(self.S.sp, dst_ap, src_ap, writes=[dst])

    def pipeline(self, steps, s_fn, p_fn, skew=2):
        pend = []
        for st in steps:
            pend.append((st, s_fn(st)))
            if len(pend) > skew:
                a, b = pend.pop(0)
                p_fn(a, b)
        while pend:
            a, b = pend.pop(0)
            p_fn(a, b)

    def arecip(self, out_buf, out_ap, in_buf, in_ap, bias=0.0):
        if bias != 0.0:
            self.act(out_ap, in_ap, AF.Ln, [in_buf], [out_buf], bias=bias)
        else:
            self.act(out_ap, in_ap, AF.Ln, [in_buf], [out_buf])
        self.act(out_ap, out_ap, AF.Exp, [out_buf], [out_buf], scale=-1.0)

    def psn(self):
        rb = self.ring_banks
        b = self.PS[rb[self.ps_i % len(rb)]]
        self.ps_i += 1
        return b

    def set_ring(self, banks):
        self.ring_banks = list(banks)

    def load_split(self, dst, dst_ap, src_ap, n, cast=True):
        A = dst_ap.shape[1]
        step = (A + n - 1) // n
        for a0 in range(0, A, step):
            a1 = min(A, a0 + step)
            if cast:
                self.load_w(dst, dst_ap[:, a0:a1], src_ap[:, a0:a1])
            else:
                self.load_f(dst, dst_ap[:, a0:a1], src_ap[:, a0:a1])

    def proj(self, ps, w, wcols, M, hT, hcols, N):
        for k in range(8):
            self.mm(ps, ps[0:M, 0:N], w[:, k, wcols], hT[:, k, hcols], k == 0, k == 7, [w, hT])

    def norm_chunk(self, c, gcol, dst, dcol0, f32_out=False):
        S = self.S
        xb = self.xTb[c]
        cols = slice(c * 512, (c + 1) * 512)
        ss = self.psn()
        for k in range(8):
            sq = self.sqring()
            self.act(sq[:, :], self.xT[:, k, cols], AF.Square, [xb], [sq])
            self.mm(ss, ss[:, :], self.onesB[:, :], sq[:, :], k == 0, k == 7, [self.onesB, sq])
        rs = self.rsring()
        self.act(rs[:, :], ss[:, :], AF.Sqrt, [ss], [rs], bias=EPS, scale=1.0 / D)
        self.recip(rs[:, :], rs[:, :], [rs], [rs])
        for k in range(8):
            self.stt(dst[:, k, dcol0:dcol0 + 512], self.xT[:, k, cols], self.gains[:, gcol + k:gcol + k + 1], rs[:, :],
                     ALU.mult, ALU.mult, [xb, self.gains, rs], [dst])

    def rope_comb(self, psA, psB, Ct, St, tcols, M, dst, dst_ap):
        t1 = self.t1ring()
        t2 = self.t1ring()
        self.tt(t1[0:M, :], psA[0:M, :], Ct[0:M, tcols], ALU.mult, [psA, Ct], [t1])
        self.tt(t2[0:M, :], psB[0:M, :], St[0:M, tcols], ALU.mult, [psB, St], [t2])
        self.tt(dst_ap, t1[0:M, :], t2[0:M, :], ALU.add, [t1, t2], [dst])

    def pnorm(self, src_ps, gain_ap, gain_buf, dst, dst_ap, nfeat, ss_bank=None):
        sq = self.sqring()
        self.act(sq[:, :], src_ps[:, :], AF.Square, [src_ps], [sq])
        ss = self.psn() if ss_bank is None else ss_bank
        self.mm(ss, ss[:, :], self.onesB[:, :], sq[:, :], True, True, [self.onesB, sq])
        rs = self.rsring()
        self.act(rs[:, :], ss[:, :], AF.Sqrt, [ss], [rs], bias=EPS, scale=1.0 / nfeat)
        self.recip(rs[:, :], rs[:, :], [rs], [rs])
        self.stt(dst_ap, src_ps[:, :], gain_ap, rs[:, :], ALU.mult, ALU.mult, [src_ps, gain_buf, rs], [dst])

    def out_proj(self, wo, koff, oT, c):
        xb = self.xTb[c]
        for m in range(8):
            ps = self.psn()
            for k in range(4):
                self.mm(ps, ps[:, :], wo[:, koff + k, m * 128:(m + 1) * 128], oT[:, k, :], k == 0, k == 3, [wo, oT])
            self.tt(self.xT[:, m, c * 512:(c + 1) * 512], self.xT[:, m, c * 512:(c + 1) * 512], ps[:, :], ALU.add, [xb, ps], [xb])

    def load_x(self, s):
        for tt in range(NTT):
            xin = self.xinring()
            self.S.dma(self.S.sp, xin[:, :], self.x[s, tt * 128:(tt + 1) * 128, :], writes=[xin])
            xb = self.xTb[tt // 4]
            for half in range(2):
                ps = self.psn()
                for cc in range(4):
                    c8 = half * 4 + cc
                    self.S.op(self.S.pe, lambda e, ps=ps, cc=cc, c8=c8, xin=xin: e.transpose(
                        out=ps[:, cc * 128:(cc + 1) * 128], in_=xin[:, c8 * 128:(c8 + 1) * 128], identity=self.identF[:, :]),
                        reads=[xin, self.identF], writes=[ps])
                self.act(self.xT[:, half * 4:half * 4 + 4, tt * 128:(tt + 1) * 128],
                         ps[:, :].rearrange("p (c t) -> p c t", c=4), AF.Copy, [ps], [xb])

    def store_out(self, s):
        yT = self.sb("yT", [128, 8, 512], F32)
        for c in range(NCH):
            if self.final == "raw":
                for k in range(8):
                    self.vcopy(yT[:, k, :], self.xT[:, k, c * 512:(c + 1) * 512], [self.xTb[c]], [yT])
            else:
                self.norm_chunk(c, 64, yT, 0)
            for t4 in range(4):
                tt = c * 4 + t4
                yo = self.xinring()
                for half in range(2):
                    ps = self.psn()
                    for cc in range(4):
                        c8 = half * 4 + cc
                        self.S.op(self.S.pe, lambda e, ps=ps, cc=cc, c8=c8: e.transpose(
                            out=ps[:, cc * 128:(cc + 1) * 128], in_=yT[:, c8, t4 * 128:(t4 + 1) * 128], identity=self.identF[:, :]),
                            reads=[yT, self.identF], writes=[ps])
                    self.act(yo[:, half * 512:(half + 1) * 512], ps[:, :], AF.Copy, [ps], [yo])
                self.S.dma(self.S.sp, self.out[s, tt * 128:(tt + 1) * 128, :], yo[:, :], reads=[yo], writes=[self.outbuf])

    def ffn(self, l):
        self.push()
        hT = self.sb("hTf", [128, 8, 1024], BF16)
        actb = self.sb("actb", [128, NFF, 1024], BF16)
        wgr = self.ring("wg", 8, [128, 8, 128], BF16)
        wur = self.ring("wu", 8, [128, 8, 128], BF16)
        wdr = self.ring("wd", 3, [128, NFF, 128], BF16)
        sgr = self.ring("sg", 2, [128, 512], F32)
        wg_d = self.dram[f"f{l}_wg"]
        wu_d = self.dram[f"f{l}_wu"]
        wd_d = self.dram[f"f{l}_wd"]
        for half in range(2):
            for sub in range(2):
                self.norm_chunk(half * 2 + sub, 32 + l * 8, hT, sub * 512)
            for j in range(NFF):
                wg = wgr()
                wu = wur()
                self.load_w(wg, wg[:, :, :], wg_d[j].rearrange("p (k m) -> p k m", k=8))
                self.load_w(wu, wu[:, :, :], wu_d[j].rearrange("p (k m) -> p k m", k=8))
                for sub in range(2):
                    hc = slice(sub * 512, (sub + 1) * 512)
                    pg = self.psn()
                    pu = self.psn()
                    self.proj(pg, wg, slice(0, 128), 128, hT, hc, 512)
                    self.proj(pu, wu, slice(0, 128), 128, hT, hc, 512)
                    sg = sgr()
                    self.act(sg[:, :], pg[:, :], AF.Silu, [pg], [sg])
                    self.tt(actb[:, j, hc], sg[:, :], pu[:, :], ALU.mult, [sg, pu], [actb])
            for m in range(8):
                wd = wdr()
                self.load_split(wd, wd[:, :, :], wd_d[m].rearrange("p (j n) -> p j n", j=NFF), 2)
                for sub in range(2):
                    c = half * 2 + sub
                    hc = slice(sub * 512, (sub + 1) * 512)
                    ps = self.psn()
                    for j in range(NFF):
                        self.mm(ps, ps[:, :], wd[:, j, :], actb[:, j, hc], j == 0, j == NFF - 1, [wd, actb])
                    xc = slice(c * 512, (c + 1) * 512)
                    self.tt(self.xT[:, m, xc], self.xT[:, m, xc], ps[:, :], ALU.add, [self.xTb[c], ps], [self.xTb[c]])
        self.pop()

    def dsa(self, l, oTaF):
        j = l // 2
        p = f"e{j}_"
        dr = self.dram
        S = self.S
        gcol = l * 8
        self.push()
        ckvT = self.sb("ckvT", [128, T], BF16)
        ckvTok = self.sb("ckvTok", [128, NTT, 128], BF16)
        krT = self.sb("krT", [64, T], BF16)
        ikT = self.sb("ikT", [128, T], BF16)
        kvg = self.sb("kvg", [128, 1], F32)
        self.load_f(kvg, kvg[:, :], dr[p + "kvg"][:, :])
        self.push()
        hT = self.sb("hTk", [128, 8, T], BF16)
        wckv = self.sb("wckv", [128, 8, 128], BF16)
        wkr = self.sb("wkr", [128, 8, 64], BF16)
        wkrs = self.sb("wkrs", [128, 8, 64], BF16)
        wik = self.sb("wik", [128, 8, 128], BF16)
        wiks = self.sb("wiks", [128, 8, 128], BF16)
        for wt, nm, mc in ((wckv, "ckv", 128), (wkr, "kr", 64), (wkrs, "krs", 64), (wik, "ik", 128), (wiks, "iks", 128)):
            self.load_w(wt, wt[:, :, :], dr[p + nm][0].rearrange("p (k m) -> p k m", k=8))
        C64 = self.sb("C64", [128, T], F32)
        S64 = self.sb("S64", [128, T], F32)
        C32 = self.sb("C32", [64, T], F32)
        S32 = self.sb("S32", [64, T], F32)
        for tb, nm in ((C64, "C64"), (S64, "S64"), (C32, "C32"), (S32, "S32")):
            for hf in range(2):
                self.load_f(tb, tb[:, hf * 1024:(hf + 1) * 1024], dr[nm][:, hf * 1024:(hf + 1) * 1024])
        for c in range(NCH):
            self.norm_chunk(c, gcol, hT, c * 512)
        for c in range(NCH):
            cols = slice(c * 512, (c + 1) * 512)
            ps = self.psn()
            self.proj(ps, wckv, slice(0, 128), 128, hT, cols, 512)
            self.pnorm(ps, kvg[:, 0:1], kvg, ckvT, ckvT[:, cols], 128)
            pa = self.psn()
            pb = self.psn()
            self.proj(pa, wkr, slice(0, 64), 64, hT, cols, 512)
            self.proj(pb, wkrs, slice(0, 64), 64, hT, cols, 512)
            self.rope_comb(pa, pb, C32, S32, cols, 64, krT, krT[:, cols])
            pa = self.psn()
            pb = self.psn()
            self.proj(pa, wik, slice(0, 128), 128, hT, cols, 512)
            self.proj(pb, wiks, slice(0, 128), 128, hT, cols, 512)
            self.rope_comb(pa, pb, C64, S64, cols, 128, ikT, ikT[:, cols])
        for t8 in range(2):
            ps = self.psn()
            psb = ps[:, :].bitcast(BF16)
            for i in range(8):
                tt = t8 * 8 + i
                S.op(S.pe, lambda e, psb=psb, i=i, tt=tt: e.transpose(out=psb[:, i * 128:(i + 1) * 128], in_=ckvT[:, tt * 128:(tt + 1) * 128],
                                                                     identity=self.identB[:, :]), reads=[ckvT, self.identB], writes=[ps])
            self.act(ckvTok[:, t8 * 8:(t8 + 1) * 8, :], psb.rearrange("p (i t) -> p i t", i=8), AF.Copy, [ps], [ckvTok])
        self.pop()
        self.push()
        hTc = self.sb("hTc", [128, 8, 512], BF16)
        qlatT = self.sb("qlatT", [128, 8, 512], BF16)
        qrT = self.sb("qrT", [64, 4, 512], BF16)
        iqT = self.sb("iqT", [128, 4, 512], BF16)
        iwt = self.sb("iwt", [128, 4, 8], F32)
        Isc = self.sb("Isc", [128, T], F32)
        Wk = self.sb("Wk", [128, T], F32)
        Mq = self.sb("Mq", [128, T], BF16)
        MT = self.sb("MT", [128, NTT, 512], BF16)
        wuk = self.sb("wuk", [128, 4, 128], BF16)
        wuvp = self.sb("wuvp", [128, 8, 128], BF16)
        wiw = self.sb("wiw", [128, 8, 8], BF16)
        self.load_w(wuk, wuk[:, :, :], dr[p + "uk"][:, :].rearrange("p (j c) -> p j c", j=4))
        self.load_w(wuvp, wuvp[:, :, :], dr[p + "uvp"][:, :].rearrange("p (h c) -> p h c", h=8))
        self.load_w(wiw, wiw[:, :, :], dr[p + "iw"][0].rearrange("p (k m) -> p k m", k=8))
        wr = self.ring("wq", 3, [128, 8, 128], BF16)
        C64c = self.sb("C64c", [128, 512], F32)
        S64c = self.sb("S64c", [128, 512], F32)
        C32c = C64c
        S32c = S64c
        qnr = self.ring("qn", 2, [128, 512], BF16)
        relur = self.ring("relu", 2, [128, 512], F32)
        m8r = self.ring("m8", 3, [128, 8], F32)
        Ptr = self.ring("Pt", 3, [128, 512], BF16)
        olr = self.ring("oln", 2, [128, 512], BF16)
        c0 = slice(0, 512)
        for c in range(NCH):
            cols = slice(c * 512, (c + 1) * 512)
            self.norm_chunk(c, gcol, hTc, 0)
            self.load_f(C32c, C32c[0:64, :], dr["C32"][:, cols])
            self.load_f(S32c, S32c[0:64, :], dr["S32"][:, cols])
            for jj in range(4):
                wq = wr()
                self.load_w(wq, wq[:, :, :], dr[p + "qn"][jj].rearrange("p (k m) -> p k m", k=8))
                ps = self.psn()
                self.proj(ps, wq, slice(0, 128), 128, hTc, c0, 512)
                qn = qnr()
                self.act(qn[:, :], ps[:, :], AF.Copy, [ps], [qn])
                for hh in range(2):
                    h = 2 * jj + hh
                    ps2 = self.psn()
                    self.mm(ps2, ps2[:, :], wuk[64 * hh:64 * hh + 64, jj, :], qn[64 * hh:64 * hh + 64, :], True, True, [wuk, qn])
                    self.act(qlatT[:, h, :], ps2[:, :], AF.Copy, [ps2], [qlatT])
            for jj in range(4):
                wq = wr()
                wqs = wr()
                self.load_w(wq, wq[:, :, 0:64], dr[p + "qr"][jj].rearrange("p (k m) -> p k m", k=8))
                self.load_w(wqs, wqs[:, :, 0:64], dr[p + "qrs"][jj].rearrange("p (k m) -> p k m", k=8))
                pa = self.psn()
                pb = self.psn()
                self.proj(pa, wq, slice(0, 64), 64, hTc, c0, 512)
                self.proj(pb, wqs, slice(0, 64), 64, hTc, c0, 512)
                self.rope_comb(pa, pb, C32c, S32c, c0, 64, qrT, qrT[:, jj, :])
            self.load_f(C64c, C64c[:, :], dr["C64"][:, cols])
            self.load_f(S64c, S64c[:, :], dr["S64"][:, cols])
            for jj in range(4):
                wq = wr()
                wqs = wr()
                self.load_w(wq, wq[:, :, :], dr[p + "iq"][jj].rearrange("p (k m) -> p k m", k=8))
                self.load_w(wqs, wqs[:, :, :], dr[p + "iqs"][jj].rearrange("p (k m) -> p k m", k=8))
                pa = self.psn()
                pb = self.psn()
                self.proj(pa, wq, slice(0, 128), 128, hTc, c0, 512)
                self.proj(pb, wqs, slice(0, 128), 128, hTc, c0, 512)
                self.rope_comb(pa, pb, C64c, S64c, c0, 128, iqT, iqT[:, jj, :])
            for qb in range(4):
                ps = self.psn()
                for k in range(8):
                    self.mm(ps, ps[:, 0:8], hTc[:, k, qb * 128:(qb + 1) * 128], wiw[:, k, :], k == 0, k == 7, [hTc, wiw])
                self.vcopy(iwt[:, qb, :], ps[:, 0:8], [ps], [iwt])
            for qb in range(4):
                gq = 4 * c + qb
                n = (gq + 1) * 128
                qsl = slice(qb * 128, (qb + 1) * 128)
                nkc = (n + 511) // 512
                for kc in range(nkc):
                    ncol = min(512, n - kc * 512)
                    ks = slice(kc * 512, kc * 512 + ncol)
                    for h in range(8):
                        hh = h % 2
                        jj = h // 2
                        ps = self.psn()
                        self.mm(ps, ps[:, 0:ncol], iqT[64 * hh:64 * hh + 64, jj, qsl], ikT[64 * hh:64 * hh + 64, ks], True, True, [iqT, ikT])
                        rl = relur()
                        self.act(rl[:, 0:ncol], ps[:, 0:ncol], AF.Relu, [ps], [rl])
                        if h == 0:
                            self.ts(Isc[:, ks], rl[:, 0:ncol], iwt[:, qb, 0:1], ALU.mult, [rl, iwt], [Isc])
                        else:
                            self.stt(Isc[:, ks], rl[:, 0:ncol], iwt[:, qb, h:h + 1], Isc[:, ks], ALU.mult, ALU.add, [rl, iwt, Isc], [Isc])
                dsl = slice(gq * 128, n)
                self.tt(Isc[:, dsl], Isc[:, dsl], self.negcaus[:, :], ALU.add, [Isc, self.negcaus], [Isc])
                if n > 256:
                    src = Isc
                    m8 = None
                    for r in range(32):
                        m8 = m8r()
                        S.op(S.dve, lambda e, m8=m8, src=src: e.max(out=m8[:, :], in_=src[:, 0:n]), reads=[src], writes=[m8])
                        if r < 31:
                            S.op(S.dve, lambda e, m8=m8, src=src: e.match_replace(out=Wk[:, 0:n], in_to_replace=m8[:, :], in_values=src[:, 0:n],
                                                                                  imm_value=NEGREP), reads=[src, m8], writes=[Wk])
                            src = Wk
                    self.ts(Mq[:, 0:n], Isc[:, 0:n], m8[:, 7:8], ALU.is_ge, [Isc, m8], [Mq])
                    kt = 0
                    while kt <= gq:
                        nb = min(8, gq + 1 - kt)
                        ps = self.psn()
                        psb = ps[:, :].bitcast(BF16)
                        for i in range(nb):
                            S.op(S.pe, lambda e, psb=psb, i=i, kt=kt: e.transpose(out=psb[:, i * 128:(i + 1) * 128],
                                                                                 in_=Mq[:, (kt + i) * 128:(kt + i + 1) * 128], identity=self.identB[:, :]),
                                 reads=[Mq, self.identB], writes=[ps])
                        self.act(MT[:, kt:kt + nb, qsl], psb[:, 0:nb * 128].rearrange("p (i t) -> p i t", i=nb), AF.Copy, [ps], [MT])
                        kt += nb
                else:
                    for kt in range(gq):
                        self.vcopy(MT[:, kt, qsl], self.onesB[:, :], [self.onesB], [MT])
                    self.vcopy(MT[:, gq, qsl], self.causTB[:, :], [self.causTB], [MT])
            nkt = 4 * c + 4
            self.set_ring([5, 6, 7])

            def s_fn(st):
                h, kt = st
                hh = h % 2
                ksl = slice(kt * 128, (kt + 1) * 128)
                cs = slice(128 * max(0, kt - 4 * c), 512)
                Sps = self.psn()
                self.mm(Sps, Sps[:, cs], ckvT[:, ksl], qlatT[:, h, cs], True, False, [ckvT, qlatT])
                self.mm(Sps, Sps[:, cs], krT[32 * hh:32 * hh + 32, ksl], qrT[32 * hh:32 * hh + 32, h // 2, cs], False, True, [krT, qrT])
                return Sps

            def p_fn(st, Sps):
                h, kt = st
                hh = h % 2
                cs = slice(128 * max(0, kt - 4 * c), 512)
                Ops = self.PS[2 * hh]
                Dps = self.PS[2 * hh + 1]
                Pt = Ptr()
                self.act(Pt[:, cs], Sps[:, cs], AF.Exp, [Sps], [Pt], scale=A_SCALE)
                self.tt(Pt[:, cs], Pt[:, cs], MT[:, kt, cs], ALU.mult, [Pt, MT], [Pt])
                self.mm(Ops, Ops[:, cs], ckvTok[:, kt, :], Pt[:, cs], kt == 0, kt == nkt - 1, [ckvTok, Pt])
                self.mm(Dps, Dps[:, cs], self.onesB[:, :], Pt[:, cs], kt == 0, kt == nkt - 1, [self.onesB, Pt])
                if kt == nkt - 1:
                    rd = self.rsring()
                    self.arecip(rd, rd[:, :], Dps, Dps[:, :])
                    ol = olr()
                    self.tt(ol[:, :], Ops[:, :], rd[:, :], ALU.mult, [Ops, rd], [ol])
                    Opair = self.PS[4]
                    self.mm(Opair, Opair[:, :], wuvp[:, h, :], ol[:, :], hh == 0, hh == 1, [wuvp, ol])
                    if hh == 1:
                        self.act(oTaF[:, h // 2, cols], Opair[:, :], AF.Copy, [Opair], [oTaF])

            self.pipeline([(h, kt) for h in range(8) for kt in range(nkt)], s_fn, p_fn)
            self.set_ring(range(8))
        self.pop()
        self.pop()

    def diff(self, l, oTaF, skip=False):
        j = l // 2
        p = f"e{j}_"
        dr = self.dram
        S = self.S
        gcol = l * 8
        if skip:
            self.push()
            oTb = self.sb("oTb", [128, 4, 512], BF16)
            wob = self.sb("wob", [128, 8, 1024], BF16)
            self.load_split(wob, wob[:, :, :], dr[p + "wo"][:, :].rearrange("p (k n) -> p k n", k=8), 8)
            S.op(S.dve, lambda e: e.memset(oTb[:, :, :], 0.0), writes=[oTb])
            for c in range(NCH):
                cols = slice(c * 512, (c + 1) * 512)
                xb = self.xTb[c]
                for m in range(8):
                    ps = self.psn()
                    for k in range(8):
                        rhs = oTaF[:, k, cols] if k < 4 else oTb[:, k - 4, :]
                        self.mm(ps, ps[:, :], wob[:, k, m * 128:(m + 1) * 128], rhs, k == 0, k == 7, [wob, oTaF, oTb])
                    self.tt(self.xT[:, m, cols], self.xT[:, m, cols], ps[:, :], ALU.add, [xb, ps], [xb])
            self.pop()
            return
        lam_init = 0.8 - 0.6 * math.exp(-0.3 * l)
        self.push()
        kbT = self.sb("kbT", [128, 4, T], BF16)
        vbTok = self.sb("vbTok", [128, NTT, 512], BF16)
        lam = self.sb("lam", [1, 256], F32)
        self.load_f(lam, lam[:, :], dr[p + "lam"][:, :])
        sub = self.sb("sub", [128, 1], F32)
        self.load_f(sub, sub[:, :], dr[p + "sub"][:, :])
        pr = self.sb("lpr", [1, 128], F32)
        sm = self.sb("lsm", [1, 2], F32)
        self.tt(pr[:, 0:64], lam[:, 0:64], lam[:, 64:128], ALU.mult, [lam], [pr])
        self.tt(pr[:, 64:128], lam[:, 128:192], lam[:, 192:256], ALU.mult, [lam], [pr])
        S.op(S.dve, lambda e: e.reduce_sum(out=sm[:, 0:1], in_=pr[:, 0:64], axis=AX.X), reads=[pr], writes=[sm])
        S.op(S.dve, lambda e: e.reduce_sum(out=sm[:, 1:2], in_=pr[:, 64:128], axis=AX.X), reads=[pr], writes=[sm])
        self.act(sm[:, :], sm[:, :], AF.Exp, [sm], [sm])
        nl = self.sb("nl", [1, 1], F32)
        self.tt(nl[:, :], sm[:, 1:2], sm[:, 0:1], ALU.subtract, [sm], [nl])
        self.ts(nl[:, :], nl[:, :], -lam_init, ALU.add, [nl], [nl])
        ps = self.psn()
        self.mm(ps, ps[:, 0:1], self.onesF[0:1, :], nl[0:1, 0:1], True, True, [self.onesF, nl])
        neglam = self.sb("neglam", [128, 1], F32)
        self.vcopy(neglam[:, :], ps[:, 0:1], [ps], [neglam])
        self.ts(sub[:, :], sub[:, :], 1.0 - lam_init, ALU.mult, [sub], [sub])
        self.push()
        hT = self.sb("hTk", [128, 8, T], BF16)
        C64 = self.sb("C64", [128, T], F32)
        S64 = self.sb("S64", [128, T], F32)
        for hf in range(2):
            self.load_f(C64, C64[:, hf * 1024:(hf + 1) * 1024], dr["C64"][:, hf * 1024:(hf + 1) * 1024])
            self.load_f(S64, S64[:, hf * 1024:(hf + 1) * 1024], dr["S64"][:, hf * 1024:(hf + 1) * 1024])
        wr = self.ring("wk", 4, [128, 8, 128], BF16)
        wvb = self.sb("wvb", [128, 8, 512], BF16)
        self.load_split(wvb, wvb[:, :, :], dr[p + "vb"][0].rearrange("p (k m) -> p k m", k=8), 4)
        for c in range(NCH):
            self.norm_chunk(c, gcol, hT, c * 512)
        for jj in range(4):
            wk = wr()
            wks = wr()
            self.load_w(wk, wk[:, :, :], dr[p + "kb"][jj].rearrange("p (k m) -> p k m", k=8))
            self.load_w(wks, wks[:, :, :], dr[p + "kbs"][jj].rearrange("p (k m) -> p k m", k=8))
            for c in range(NCH):
                cols = slice(c * 512, (c + 1) * 512)
                pa = self.psn()
                pb = self.psn()
                self.proj(pa, wk, slice(0, 128), 128, hT, cols, 512)
                self.proj(pb, wks, slice(0, 128), 128, hT, cols, 512)
                self.rope_comb(pa, pb, C64, S64, cols, 128, kbT, kbT[:, jj, cols])
        for tt in range(NTT):
            ps = self.psn()
            for k in range(8):
                self.mm(ps, ps[:, :], hT[:, k, tt * 128:(tt + 1) * 128], wvb[:, k, :], k == 0, k == 7, [hT, wvb])
            self.act(vbTok[:, tt, :], ps[:, :], AF.Copy, [ps], [vbTok])
        self.pop()
        self.push()
        hTc = self.sb("hTc", [128, 8, 512], BF16)
        qbT = self.sb("qbT", [128, 4, 512], BF16)
        oTb = self.sb("oTb", [128, 4, 512], BF16)
        wob = self.sb("wob", [128, 8, 1024], BF16)
        self.load_split(wob, wob[:, :, :], dr[p + "wo"][:, :].rearrange("p (k n) -> p k n", k=8), 8)
        wr = self.ring("wq", 4, [128, 8, 128], BF16)
        C64c = self.sb("C64c", [128, 512], F32)
        S64c = self.sb("S64c", [128, 512], F32)
        Ptr = self.ring("Pt", 3, [128, 512], BF16)
        a1r = self.ring("a1", 2, [128, 512], F32)
        dfr = self.ring("df", 2, [128, 512], F32)
        c0 = slice(0, 512)
        for c in range(NCH):
            cols = slice(c * 512, (c + 1) * 512)
            self.norm_chunk(c, gcol, hTc, 0)
            self.load_f(C64c, C64c[:, :], dr["C64"][:, cols])
            self.load_f(S64c, S64c[:, :], dr["S64"][:, cols])
            for jj in range(4):
                wq = wr()
                wqs = wr()
                self.load_w(wq, wq[:, :, :], dr[p + "qb"][jj].rearrange("p (k m) -> p k m", k=8))
                self.load_w(wqs, wqs[:, :, :], dr[p + "qbs"][jj].rearrange("p (k m) -> p k m", k=8))
                pa = self.psn()
                pb = self.psn()
                self.proj(pa, wq, slice(0, 128), 128, hTc, c0, 512)
                self.proj(pb, wqs, slice(0, 128), 128, hTc, c0, 512)
                self.rope_comb(pa, pb, C64c, S64c, c0, 128, qbT, qbT[:, jj, :])
            nkt = 4 * c + 4
            self.set_ring([5, 6, 7])
            am = {}

            def s_fn(st):
                h, m, kt = st
                i = 2 * h + m
                hh = i % 2
                jj = i // 2
                ksl = slice(kt * 128, (kt + 1) * 128)
                cs = slice(128 * max(0, kt - 4 * c), 512)
                Sps = self.psn()
                self.mm(Sps, Sps[:, cs], kbT[64 * hh:64 * hh + 64, jj, ksl], qbT[64 * hh:64 * hh + 64, jj, cs], True, True, [kbT, qbT])
                return Sps

            def p_fn(st, Sps):
                h, m, kt = st
                cs = slice(128 * max(0, kt - 4 * c), 512)
                Ops = self.PS[2 * m]
                Dps = self.PS[2 * m + 1]
                Pt = Ptr()
                self.act(Pt[:, cs], Sps[:, cs], AF.Exp, [Sps], [Pt], scale=0.125)
                if kt >= 4 * c:
                    dg = slice(128 * (kt - 4 * c), 128 * (kt - 4 * c) + 128)
                    self.tt(Pt[:, dg], Pt[:, dg], self.causTB[:, :], ALU.mult, [Pt, self.causTB], [Pt])
                self.mm(Ops, Ops[:, cs], vbTok[:, kt, h * 128:(h + 1) * 128], Pt[:, cs], kt == 0, kt == nkt - 1, [vbTok, Pt])
                self.mm(Dps, Dps[:, cs], self.onesB[:, :], Pt[:, cs], kt == 0, kt == nkt - 1, [self.onesB, Pt])
                if kt == nkt - 1:
                    rd = self.rsring()
                    self.arecip(rd, rd[:, :], Dps, Dps[:, :])
                    a = a1r()
                    self.tt(a[:, :], Ops[:, :], rd[:, :], ALU.mult, [Ops, rd], [a])
                    am[m] = a
                    if m == 1:
                        df = dfr()
                        self.stt(df[:, :], am[1][:, :], neglam[:, 0:1], am[0][:, :], ALU.mult, ALU.add, [am[0], am[1], neglam], [df])
                        self.pnorm(df, sub[:, 0:1], sub, oTb, oTb[:, h, :], 128, ss_bank=self.PS[4])

            self.pipeline([(h, m, kt) for h in range(4) for m in range(2) for kt in range(nkt)], s_fn, p_fn)
            self.set_ring(range(8))
            xb = self.xTb[c]
            for m in range(8):
                ps = self.psn()
                for k in range(8):
                    rhs = oTaF[:, k, cols] if k < 4 else oTb[:, k - 4, :]
                    self.mm(ps, ps[:, :], wob[:, k, m * 128:(m + 1) * 128], rhs, k == 0, k == 7, [wob, oTaF, oTb])
                self.tt(self.xT[:, m, cols], self.xT[:, m, cols], ps[:, :], ALU.add, [xb, ps], [xb])
        self.pop()
        self.pop()


    def nsa(self, l):
        j = l // 2
        p = f"o{j}_"
        dr = self.dram
        S = self.S
        gcol = l * 8
        self.set_ring(range(8))
        self.push()
        ksT = self.sb("ksT", [128, 2, T], BF16)
        kwT = self.sb("kwT", [128, 2, T], BF16)
        vsTok = self.sb("vsTok", [128, NTT, 256], BF16)
        vwTok = self.sb("vwTok", [128, NTT, 256], BF16)
        kcmpT = self.sb("kcmpT", [128, 2, 128], BF16)
        vcmpTok = self.sb("vcmpTok", [128, 256], BF16)
        S.op(S.dve, lambda e: e.memset(kcmpT[:, :, :], 0.0), writes=[kcmpT])
        S.op(S.dve, lambda e: e.memset(vcmpTok[:, :], 0.0), writes=[vcmpTok])
        self.push()
        kcT = self.sb("kcT", [128, 2, T], BF16)
        vcT = self.sb("vcT", [128, 2, T], BF16)
        self.push()
        hT = self.sb("hTk", [128, 8, T], BF16)
        C64 = self.sb("C64", [128, T], F32)
        S64 = self.sb("S64", [128, T], F32)
        for hf in range(2):
            self.load_f(C64, C64[:, hf * 1024:(hf + 1) * 1024], dr["C64"][:, hf * 1024:(hf + 1) * 1024])
            self.load_f(S64, S64[:, hf * 1024:(hf + 1) * 1024], dr["S64"][:, hf * 1024:(hf + 1) * 1024])
        wr = self.ring("wk", 4, [128, 8, 128], BF16)
        for c in range(NCH):
            self.norm_chunk(c, gcol, hT, c * 512)
        for nm, dst, roped in (("kc", kcT, True), ("ks", ksT, True), ("kw", kwT, True), ("vc", vcT, False)):
            for gg in range(2):
                wk = wr()
                self.load_w(wk, wk[:, :, :], dr[p + nm][gg].rearrange("p (k m) -> p k m", k=8))
                if roped:
                    wks = wr()
                    self.load_w(wks, wks[:, :, :], dr[p + nm + "s"][gg].rearrange("p (k m) -> p k m", k=8))
                for c in range(NCH):
                    cols = slice(c * 512, (c + 1) * 512)
                    pa = self.psn()
                    self.proj(pa, wk, slice(0, 128), 128, hT, cols, 512)
                    if roped:
                        pb = self.psn()
                        self.proj(pb, wks, slice(0, 128), 128, hT, cols, 512)
                        self.rope_comb(pa, pb, C64, S64, cols, 128, dst, dst[:, gg, cols])
                    else:
                        self.act(dst[:, gg, cols], pa[:, :], AF.Copy, [pa], [dst])
        wv = self.sb("wv", [128, 8, 256], BF16)
        for nm, dst in (("vs", vsTok), ("vw", vwTok)):
            self.load_split(wv, wv[:, :, :], dr[p + nm][0].rearrange("p (k m) -> p k m", k=8), 2)
            for tt in range(NTT):
                ps = self.psn()
                for k in range(8):
                    self.mm(ps, ps[:, 0:256], hT[:, k, tt * 128:(tt + 1) * 128], wv[:, k, :], k == 0, k == 7, [hT, wv])
                self.act(dst[:, tt, :], ps[:, 0:256], AF.Copy, [ps], [dst])
        self.pop()
        self.push()
        w1 = self.sb("w1", [128, 32, 128], BF16)
        peT = self.sb("peT", [128, 32], BF16)
        w2 = self.sb("w2", [128, 64], BF16)
        cb = self.sb("cb", [128, 1], F32)
        gx = self.ring("gx", 2, [128, 128], F32)
        gu = self.ring("gu", 2, [128, 128], F32)
        Gb = self.ring("Gb", 2, [128, 128], BF16)
        for kv, nm, src in ((0, "k", kcT), (1, "v", vcT)):
            self.load_split(w1, w1[:, :, :], dr[p + "w1" + nm][:, :].rearrange("p (a b) -> p a b", a=32), 4)
            self.load_w(peT, peT[:, :], dr[p + "pe" + nm][:, :])
            self.load_w(w2, w2[:, :], dr[p + "w2" + nm][:, :])
            ps = self.psn()
            for pp in range(32):
                self.mm(ps, ps[:, 0:1], w1[0:64, pp, :], peT[0:64, pp:pp + 1], pp == 0, pp == 31, [w1, peT])
            self.vcopy(cb[:, :], ps[:, 0:1], [ps], [cb])
            for g in range(4):
                hf = g % 2
                gg = g // 2
                ps = self.psn()
                for pp in range(32):
                    self.mm(ps, ps[:, 0:127], w1[64 * hf:64 * hf + 64, pp, :], src[64 * hf:64 * hf + 64, gg, pp:pp + 2017:16],
                            pp == 0, pp == 31, [w1, src])
                x_ = gx()
                u_ = gu()
                G = Gb()
                self.ts(x_[:, 0:127], ps[:, 0:127], cb[:, 0:1], ALU.add, [ps, cb], [x_])
                self.tt(u_[:, 0:127], x_[:, 0:127], x_[:, 0:127], ALU.mult, [x_], [u_])
                self.ts(u_[:, 0:127], u_[:, 0:127], 0.044715, ALU.mult, [u_], [u_], s2=1.0, op1=ALU.add)
                self.tt(u_[:, 0:127], u_[:, 0:127], x_[:, 0:127], ALU.mult, [u_, x_], [u_])
                self.act(u_[:, 0:127], u_[:, 0:127], AF.Sigmoid, [u_], [u_], scale=1.5957691216057308)
                self.tt(G[:, 0:127], u_[:, 0:127], x_[:, 0:127], ALU.mult, [u_, x_], [G])
                ps2 = self.psn()
                if kv == 0:
                    self.mm(ps2, ps2[0:64, 0:127], w2[:, :], G[:, 0:127], True, True, [w2, G])
                    self.act(kcmpT[64 * hf:64 * hf + 64, gg, 0:127], ps2[0:64, 0:127], AF.Copy, [ps2], [kcmpT])
                else:
                    self.mm(ps2, ps2[0:127, 0:64], G[:, 0:127], w2[:, :], True, True, [w2, G])
                    self.act(vcmpTok[0:127, g * 64:(g + 1) * 64], ps2[0:127, 0:64], AF.Copy, [ps2], [vcmpTok])
        self.pop()
        self.pop()
        self.push()
        maskc = self.sb("maskc", [128, T], BF16)
        SELADD = self.sb("SELADD", [128, 16, 32], F32)
        EXPB = self.sb("EXPB", [32, 16, 128], BF16)
        SELB = self.sb("SELB", [48, 48, 64], BF16)
        ovl = self.sb("ovl", [128, 32], BF16)
        antiTB = self.sb("antiTB", [128, 128], BF16)
        self.load_split(maskc, maskc[:, :].rearrange("p (a b) -> p a b", a=2), dr["maskcmpT"][:, :].rearrange("p (a b) -> p a b", a=2), 2)
        self.load_f(SELADD, SELADD[:, :, :], dr["SELADD"][:, :].rearrange("p (a b) -> p a b", a=16))
        self.load_w(EXPB, EXPB[:, :, :], dr["EXPB"][:, :].rearrange("p (a b) -> p a b", a=16))
        self.load_w(SELB, SELB[:, :, :], dr["SELB"][:, :].rearrange("p (a b) -> p a b", a=48))
        self.load_w(ovl, ovl[:, :], dr["ovl"][:, :])
        self.load_w(antiTB, antiTB[:, :], dr["antiT"][:, :])
        hTc = self.sb("hTc", [128, 8, 512], BF16)
        qT = self.sb("qT", [128, 8, 512], BF16)
        MS = self.sb("MS", [128, NTT, 512], BF16)
        oT = self.sb("oT", [128, 8, 512], BF16)
        wr = self.ring("wq", 3, [128, 8, 128], BF16)
        C64c = self.sb("C64c", [128, 512], F32)
        S64c = self.sb("S64c", [128, 512], F32)
        gs = self.sb("gs", [48, 512], F32)
        ghi = self.sb("ghi", [48, 512], BF16)
        glo = self.sb("glo", [48, 512], BF16)
        wg = self.sb("wg", [128, 8, 48], BF16)
        self.load_w(wg, wg[:, :, :], dr[p + "g"][0].rearrange("p (k m) -> p k m", k=8))
        Ptr = self.ring("Pt", 2, [128, 512], BF16)
        Er = Ptr
        ocmp = self.sb("ocmp", [64, 4, 512], BF16)
        accr = self.ring("acc", 2, [64, 512], F32)
        cfr = self.ring("cf", 1, [64, 512], F32)
        tmr = self.ring("tm", 1, [64, 512], F32)
        imp2r = self.ring("imp2", 2, [128, 32], F32)
        w2r = self.ring("w2r", 2, [128, 32], F32)
        m8r = self.ring("m8", 4, [128, 8], F32)
        selr = self.ring("sel", 2, [128, 32], BF16)
        selT = self.sb("selT", [32, 512], BF16)
        impS = self.sb("impS", [32, 512], F32)
        c0 = slice(0, 512)
        for c in range(NCH):
            cols = slice(c * 512, (c + 1) * 512)
            self.set_ring(range(8))
            self.norm_chunk(c, gcol, hTc, 0)
            self.load_f(C64c, C64c[:, :], dr["C64"][:, cols])
            self.load_f(S64c, S64c[:, :], dr["S64"][:, cols])
            for qi in range(8):
                wq = wr()
                wqs = wr()
                self.load_w(wq, wq[:, :, :], dr[p + "q"][qi].rearrange("p (k m) -> p k m", k=8))
                self.load_w(wqs, wqs[:, :, :], dr[p + "qs"][qi].rearrange("p (k m) -> p k m", k=8))
                pa = self.psn()
                pb = self.psn()
                self.proj(pa, wq, slice(0, 128), 128, hTc, c0, 512)
                self.proj(pb, wqs, slice(0, 128), 128, hTc, c0, 512)
                self.rope_comb(pa, pb, C64c, S64c, c0, 128, qT, qT[:, qi, :])
            ps = self.psn()
            self.proj(ps, wg, slice(0, 48), 48, hTc, c0, 512)
            self.act(gs[:, :], ps[0:48, :], AF.Sigmoid, [ps], [gs])
            self.vcopy(ghi[:, :], gs[:, :], [gs], [ghi])
            self.tt(glo[:, :], gs[:, :], ghi[:, :], ALU.subtract, [gs, ghi], [glo])
            self.set_ring([5, 6, 7])
            ncmp = 32 * c + 31
            nkt = 4 * c + 4
            for g in range(4):
                hf = g % 2
                gg = g // 2
                hs = slice(64 * hf, 64 * hf + 64)
                imp = self.PS[0]
                for i in range(4):
                    qi = gg * 4 + i
                    Sps = self.psn()
                    self.mm(Sps, Sps[0:ncmp, :], kcmpT[hs, gg, 0:ncmp], qT[hs, qi, :], True, True, [kcmpT, qT])
                    E = Er()
                    self.act(E[0:ncmp, :], Sps[0:ncmp, :], AF.Exp, [Sps], [E], scale=0.125)
                    self.tt(E[0:ncmp, :], E[0:ncmp, :], maskc[0:ncmp, cols], ALU.mult, [E, maskc], [E])
                    Dps = self.psn()
                    self.mm(Dps, Dps[:, :], self.onesB[0:ncmp, :], E[0:ncmp, :], True, True, [self.onesB, E])
                    rd = self.rsring()
                    self.arecip(rd, rd[:, :], Dps, Dps[:, :], bias=1e-30)
                    Pn = Ptr()
                    self.tt(Pn[0:ncmp, :], E[0:ncmp, :], rd[0:ncmp, :], ALU.mult, [E, rd], [Pn])
                    Ops = self.psn()
                    self.mm(Ops, Ops[0:64, :], vcmpTok[0:ncmp, g * 64:(g + 1) * 64], Pn[0:ncmp, :], True, True, [vcmpTok, Pn])
                    self.act(ocmp[:, i, :], Ops[0:64, :], AF.Copy, [Ops], [ocmp])
                    self.mm(imp, imp[0:32, :], ovl[0:ncmp, :], Pn[0:ncmp, :], i == 0, i == 3, [Pn, ovl])
                self.act(impS[:, :], imp[0:32, :], AF.Copy, [imp], [impS])
                for qb in range(4):
                    gq = 4 * c + qb
                    pti = self.psn()
                    S.op(S.pe, lambda e, pti=pti, qb=qb: e.transpose(out=pti[:, 0:32], in_=impS[0:32, qb * 128:(qb + 1) * 128],
                                                                    identity=self.identF[0:32, 0:32]), reads=[impS, self.identF], writes=[pti])
                    i2 = imp2r()
                    self.tt(i2[:, :], pti[:, 0:32], SELADD[:, gq, :], ALU.add, [pti, SELADD], [i2])
                    ma = m8r()
                    S.op(S.dve, lambda e, ma=ma, i2=i2: e.max(out=ma[:, :], in_=i2[:, :]), reads=[i2], writes=[ma])
                    w2_ = w2r()
                    S.op(S.dve, lambda e, ma=ma, i2=i2, w2_=w2_: e.match_replace(out=w2_[:, :], in_to_replace=ma[:, :], in_values=i2[:, :],
                                                                                 imm_value=NEGREP), reads=[i2, ma], writes=[w2_])
                    mb = m8r()
                    S.op(S.dve, lambda e, mb=mb, w2_=w2_: e.max(out=mb[:, :], in_=w2_[:, :]), reads=[w2_], writes=[mb])
                    sel = selr()
                    self.ts(sel[:, :], i2[:, :], mb[:, 7:8], ALU.is_ge, [i2, mb], [sel])
                    pst = self.psn()
                    pstb = pst[:, :].bitcast(BF16)
                    S.op(S.pe, lambda e, pstb=pstb, sel=sel: e.transpose(out=pstb[0:32, 0:128], in_=sel[:, :], identity=self.identB[:, :]),
                         reads=[sel, self.identB], writes=[pst])
                    self.act(selT[:, qb * 128:(qb + 1) * 128], pstb[0:32, 0:128], AF.Copy, [pst], [selT])
                for kt in range(nkt):
                    pm = self.psn()
                    self.mm(pm, pm[:, :], EXPB[:, kt, :], selT[:, :], True, True, [EXPB, selT])
                    if kt >= 4 * c:
                        self.tt(MS[:, kt, :], pm[:, :], self.CM[:, kt - 4 * c, :], ALU.mult, [pm, self.CM], [MS])
                    else:
                        self.act(MS[:, kt, :], pm[:, :], AF.Copy, [pm], [MS])
                self.set_ring([5, 6, 7])
                accs = {}

                def colrange(br, kt):
                    if kt >= 4 * c:
                        return slice(128 * (kt - 4 * c), 512)
                    if br == 2:
                        return slice(0, 128 * (kt - (4 * c - 4)) + 128)
                    return slice(0, 512)

                def s_fn(st):
                    i, br, n_, kt, last = st
                    qi = gg * 4 + i
                    kT_ = ksT if br == 1 else kwT
                    ksl = slice(kt * 128, (kt + 1) * 128)
                    cs = colrange(br, kt)
                    Sps = self.psn()
                    self.mm(Sps, Sps[:, cs], kT_[hs, gg, ksl], qT[hs, qi, cs], True, True, [kT_, qT])
                    return Sps

                def p_fn(st, Sps):
                    i, br, n_, kt, last = st
                    qi = gg * 4 + i
                    hd = g * 4 + i
                    vTok = vsTok if br == 1 else vwTok
                    Ops = self.PS[1 + 2 * (br - 1)]
                    Dps = self.PS[2 + 2 * (br - 1)]
                    cs = colrange(br, kt)
                    Pt = Ptr()
                    self.act(Pt[:, cs], Sps[:, cs], AF.Exp, [Sps], [Pt], scale=0.125)
                    if br == 1 and n_ == 0:
                        acc = accr()
                        accs[i] = acc
                        gps = self.PS[0]
                        r = hd * 3 + 0
                        self.mm(gps, gps[0:64, :], SELB[:, r, :], ghi[:, :], True, False, [SELB, ghi])
                        self.mm(gps, gps[0:64, :], SELB[:, r, :], glo[:, :], False, True, [SELB, glo])
                        self.tt(acc[:, :], ocmp[:, i, :], gps[0:64, :], ALU.mult, [ocmp, gps], [acc])
                    acc = accs[i]
                    if br == 1:
                        self.tt(Pt[:, cs], Pt[:, cs], MS[:, kt, cs], ALU.mult, [Pt, MS], [Pt])
                    elif kt >= 4 * c:
                        dg = slice(128 * (kt - 4 * c), 128 * (kt - 4 * c) + 128)
                        self.tt(Pt[:, dg], Pt[:, dg], self.causTB[:, :], ALU.mult, [Pt, self.causTB], [Pt])
                    else:
                        dg = slice(128 * (kt - (4 * c - 4)), 128 * (kt - (4 * c - 4)) + 128)
                        self.tt(Pt[:, dg], Pt[:, dg], antiTB[:, :], ALU.mult, [Pt, antiTB], [Pt])
                    self.mm(Ops, Ops[0:64, cs], vTok[:, kt, g * 64:(g + 1) * 64], Pt[:, cs], n_ == 0, last, [vTok, Pt])
                    self.mm(Dps, Dps[0:64, cs], self.onesB[:, 0:64], Pt[:, cs], n_ == 0, last, [self.onesB, Pt])
                    if last:
                        gps = self.PS[0]
                        r = hd * 3 + br
                        self.mm(gps, gps[0:64, :], SELB[:, r, :], ghi[:, :], True, False, [SELB, ghi])
                        self.mm(gps, gps[0:64, :], SELB[:, r, :], glo[:, :], False, True, [SELB, glo])
                        cf = cfr()
                        self.arecip(cf, cf[:, :], Dps, Dps[0:64, :])
                        self.tt(cf[:, :], cf[:, :], gps[0:64, :], ALU.mult, [cf, gps], [cf])
                        tm = tmr()
                        self.tt(tm[:, :], Ops[0:64, :], cf[:, :], ALU.mult, [Ops, cf], [tm])
                        self.tt(acc[:, :], acc[:, :], tm[:, :], ALU.add, [acc, tm], [acc])
                        if br == 2:
                            self.act(oT[hs, qi, :], acc[:, :], AF.Copy, [acc], [oT])

                steps = []
                for i in range(4):
                    for br in (1, 2):
                        kts = list(range(nkt)) if br == 1 else (list(range(4 * c - 1, max(0, 4 * c - 4) - 1, -1)) + list(range(4 * c, nkt)))
                        for n_, kt in enumerate(kts):
                            steps.append((i, br, n_, kt, n_ == len(kts) - 1))
                self.pipeline(steps, s_fn, p_fn)
                self.set_ring([5, 6, 7])
            self.set_ring(range(8))
            xb = self.xTb[c]
            for m in range(8):
                wo = wr()
                self.load_w(wo, wo[:, :, :], dr[p + "wo"][m].rearrange("p (k m) -> p k m", k=8))
                ps = self.psn()
                self.proj(ps, wo, slice(0, 128), 128, oT, c0, 512)
                self.tt(self.xT[:, m, cols], self.xT[:, m, cols], ps[:, :], ALU.add, [xb, ps], [xb])
        self.pop()
        self.pop()

    def build(self):
        nc = self.nc
        S = self.S
        dr = self.dram
        self.xT = self.sb("xT", [128, 8, T], F32)
        self.xTb = [Buf(self.xT.t) for _ in range(NCH)]
        self.gains = self.sb("gains", [128, 72], F32)
        self.identF = self.sb("identF", [128, 128], F32)
        self.identB = self.sb("identB", [128, 128], BF16)
        self.onesB = self.sb("onesB", [128, 128], BF16)
        self.onesF = self.sb("onesF", [128, 128], F32)
        self.causTB = self.sb("causTB", [128, 128], BF16)
        self.negcaus = self.sb("negcaus", [128, 128], F32)
        self.CM = self.sb("CM", [128, 4, 512], BF16)
        self.outbuf = Buf(None)
        self.PS = []
        for i in range(8):
            t = self.stk[0].enter_context(nc.psum_tensor(f"ps{i}", [128, 512], F32))
            self.PS.append(Buf(t))
        self.ps_i = 0
        self.ring_banks = list(range(8))
        self.sqring = self.ring("sq", 2, [128, 512], BF16)
        self.rsring = self.ring("rs", 2, [128, 512], F32)
        self.t1ring = self.ring("t1", 2, [128, 512], F32)
        self.load_f(self.gains, self.gains[:, :], dr["gains"][:, :])
        self.load_f(self.identF, self.identF[:, :], dr["identF"][:, :])
        self.load_f(self.negcaus, self.negcaus[:, :], dr["negcaus"][:, :])
        self.load_w(self.identB, self.identB[:, :], dr["identF"][:, :])
        self.load_w(self.causTB, self.causTB[:, :], dr["causT"][:, :])
        self.load_w(self.CM, self.CM[:, :, :], dr["CM"][:, :].rearrange("p (i q) -> p i q", i=4))
        S.op(S.dve, lambda e: e.memset(self.onesB[:, :], 1.0), writes=[self.onesB])
        S.op(S.dve, lambda e: e.memset(self.onesF[:, :], 1.0), writes=[self.onesF])
        for s in range(self.nseq):
            self.set_ring(range(8))
            self.push()
            self.xinring = self.ring("xin", 2, [128, 1024], F32)
            self.load_x(s)
            self.pop()
            for l in self.layers:
                if "mix" in self.parts:
                    if l % 2 == 0:
                        self.push()
                        oTaF = self.sb("oTaF", [128, 4, T], BF16)
                        if "dsa" in self.parts or "all" in self.parts:
                            self.dsa(l, oTaF)
                        else:
                            self.S.op(self.S.dve, lambda e: e.memset(oTaF[:, :, :], 0.0), writes=[oTaF])
                        self.diff(l, oTaF, skip=not ("diff" in self.parts or "all" in self.parts))
                        self.pop()
                    else:
                        self.nsa(l)
                if "ffn" in self.parts:
                    self.ffn(l)
            if self.final:
                self.set_ring(range(8))
                self.push()
                self.xinring = self.ring("xin", 2, [128, 1024], F32)
                self.store_out(s)
                self.pop()
        S.barrier()


def build_and_run(inputs, nseq_per_core, n_cores, layers, parts, final=True):
    consts = host_consts()
    wts = prep_weights(inputs)
    wshapes = {k: v.shape for k, v in wts.items()}
    cshapes = {k: v.shape for k, v in consts.items()}
    prog = Prog(nseq_per_core, layers, wshapes, cshapes, parts=parts, final=final)
    x = np.ascontiguousarray(inputs["x"], dtype=np.float32)
    in_maps = []
    for ci in range(n_cores):
        m = {"x": x[ci * nseq_per_core:(ci + 1) * nseq_per_core]}
        m.update(wts)
        m.update(consts)
        in_maps.append(m)
    res = run_bass_kernel_spmd(prog.nc, in_maps, core_ids=list(range(n_cores)))
    return np.concatenate([r["out"] for r in res.results], axis=0)


def kernel(**inputs):
    inputs = {k: np.asarray(v) for k, v in inputs.items()}
    return build_and_run(inputs, 2, N_CORES, list(range(DEPTH)), ("mix", "all", "ffn"), final=True).astype(np.float32)
```
